# Optimizing a Trainium2 kernel written in Bass

```python
import jax, jax.numpy as jnp
from jax import lax
import numpy as np

D_MODEL = 4096
BATCH = 1
SEQ = 8192
DEPTH = 1

HEAD_DIM = 128
N_CONF_HEADS = (D_MODEL // HEAD_DIM) // 2
N_SC_HEADS = (D_MODEL // HEAD_DIM) - N_CONF_HEADS
CONF_WIDTH = N_CONF_HEADS * HEAD_DIM
SC_WIDTH = N_SC_HEADS * HEAD_DIM
MIX_WIDTH = CONF_WIDTH + SC_WIDTH
IN_PROJ_WIDTH = 2 * CONF_WIDTH + 3 * SC_WIDTH
CONF_KERNEL = 31
SC_KERNEL = 3
D_FF = 4 * D_MODEL
RMS_EPS = 1e-6
LN_EPS = 1e-5

kernel_name = "hybrid_conformer_shortconv_sandwich_block"


def rms_norm(x, g):
    xf = x.astype(jnp.float32)
    y = xf * lax.rsqrt(jnp.mean(xf * xf, axis=-1, keepdims=True) + RMS_EPS)
    return (y * g.astype(jnp.float32)).astype(x.dtype)


def layer_norm(x, g, b):
    xf = x.astype(jnp.float32)
    mu = jnp.mean(xf, axis=-1, keepdims=True)
    xc = xf - mu
    var = jnp.mean(xc * xc, axis=-1, keepdims=True)
    y = xc * lax.rsqrt(var + LN_EPS) * g.astype(jnp.float32) + b.astype(jnp.float32)
    return y.astype(x.dtype)


def causal_depthwise_conv(u, w):
    k, c = w.shape
    return lax.conv_general_dilated(
        u, w[:, None, :].astype(u.dtype),
        window_strides=(1,), padding=[(k - 1, 0)],
        dimension_numbers=("NWC", "WIO", "NWC"),
        feature_group_count=c)


def setup_inputs(seed: int = 0) -> dict:
    key = jax.random.key(seed)
    ks = jax.random.split(key, 16)
    f32 = jnp.float32
    nrm = lambda k, shape, scale: jax.random.normal(k, shape, f32) * scale
    gain = lambda k, n: 1.0 + 0.02 * jax.random.normal(k, (n,), f32)
    return {
        "x": jax.random.normal(ks[0], (BATCH, SEQ, D_MODEL), f32),
        "norm_mix_pre": gain(ks[1], D_MODEL),
        "w_in": nrm(ks[2], (D_MODEL, IN_PROJ_WIDTH), D_MODEL ** -0.5),
        "conf_dw_w": nrm(ks[3], (CONF_KERNEL, CONF_WIDTH), CONF_KERNEL ** -0.5),
        "conf_dw_b": nrm(ks[4], (CONF_WIDTH,), 0.02),
        "conf_ln_g": gain(ks[5], CONF_WIDTH),
        "conf_ln_b": nrm(ks[6], (CONF_WIDTH,), 0.02),
        "sc_conv_w": nrm(ks[7], (SC_KERNEL, SC_WIDTH), SC_KERNEL ** -0.5),
        "w_out": nrm(ks[8], (MIX_WIDTH, D_MODEL), MIX_WIDTH ** -0.5),
        "norm_mix_post": gain(ks[9], D_MODEL),
        "norm_mlp_pre": gain(ks[10], D_MODEL),
        "w_up": nrm(ks[11], (D_MODEL, D_FF), D_MODEL ** -0.5),
        "w_down": nrm(ks[12], (D_FF, D_MODEL), D_FF ** -0.5),
        "norm_mlp_post": gain(ks[13], D_MODEL),
    }


def reference(x, norm_mix_pre, w_in, conf_dw_w, conf_dw_b, conf_ln_g, conf_ln_b,
              sc_conv_w, w_out, norm_mix_post, norm_mlp_pre, w_up, w_down, norm_mlp_post):
    h = x
    for _ in range(DEPTH):
        u = rms_norm(h, norm_mix_pre)
        proj = jnp.einsum("bsd,de->bse", u, w_in.astype(u.dtype))
        c0 = CONF_WIDTH
        conf_val = proj[..., :c0]
        conf_gate = proj[..., c0:2 * c0]
        s0 = 2 * c0
        sc_b = proj[..., s0:s0 + SC_WIDTH]
        sc_c = proj[..., s0 + SC_WIDTH:s0 + 2 * SC_WIDTH]
        sc_x = proj[..., s0 + 2 * SC_WIDTH:s0 + 3 * SC_WIDTH]

        a = conf_val * jax.nn.sigmoid(conf_gate)
        a = causal_depthwise_conv(a, conf_dw_w) + conf_dw_b.astype(a.dtype)
        a = jax.nn.silu(layer_norm(a, conf_ln_g, conf_ln_b))

        bsc = sc_b * causal_depthwise_conv(sc_c * sc_x, sc_conv_w)

        mix = jnp.concatenate([a, bsc], axis=-1)
        mix = jnp.einsum("bse,ed->bsd", mix, w_out.astype(mix.dtype))
        h = h + rms_norm(mix, norm_mix_post)

        v = rms_norm(h, norm_mlp_pre)
        hid = jnp.square(jax.nn.relu(jnp.einsum("bsd,df->bsf", v, w_up.astype(v.dtype))))
        ff = jnp.einsum("bsf,fd->bsd", hid, w_down.astype(hid.dtype))
        h = h + rms_norm(ff, norm_mlp_post)
    return h
```

```python
import numpy as np
from contextlib import ExitStack

import concourse.bass as bass
import concourse.mybir as mybir
from concourse.bass_utils import run_bass_kernel_spmd

F32 = mybir.dt.float32
BF16 = mybir.dt.bfloat16
ALU = mybir.AluOpType
AF = mybir.ActivationFunctionType

NCORES = 8
D = 4096
S = 8192
TPC = S // NCORES
T = 512
NT = TPC // T
HALO = 32
TW = T + HALO
KC = D // 128
EIN = 10240
DFF = 16384
NR = 3
RMS_EPS = 1e-6
LN_EPS = 1e-5

C_G = 0
C_DWW = 128
C_DWB = 624
C_LNG = 640
C_LNB = 656
C_SCW = 672
C_EPS = 720
NCF = 768

ENGS = ["pe", "act", "dve", "pool", "sp"]


class Buf:
    __slots__ = ("w", "r", "name")

    def __init__(self, name=""):
        self.w = None
        self.r = {}
        self.name = name


class Sched:
    def __init__(self):
        self.ops = {e: [] for e in ENGS}
        self.cnt = {e: 0 for e in ENGS}
        self.dma_cnt = {}

    def _deps(self, reads, writes, deps):
        toks = {}

        def add(tok):
            if tok is None:
                return
            k, v = tok
            if toks.get(k, 0) < v:
                toks[k] = v
        for d in deps:
            if isinstance(d, dict):
                for k, v in d.items():
                    add((k, v))
            else:
                add(d)
        for b in reads:
            add(b.w)
        for b in writes:
            add(b.w)
            for k, v in b.r.items():
                add((k, v))
        return toks

    def _commit(self, tok, reads, writes):
        k, v = tok
        for b in reads:
            if b.r.get(k, 0) < v:
                b.r[k] = v
        for b in writes:
            b.w = tok
            b.r = {}

    def op(self, eng, fn, reads=(), writes=(), signal=True, deps=()):
        toks = self._deps(reads, writes, deps)
        if eng == "pe":
            toks.pop("pe", None)
        tok = (eng, self.cnt[eng] + 1)
        if signal:
            self.cnt[eng] += 1
        self.ops[eng].append((fn, toks, (eng, 1) if signal else None))
        self._commit(tok, reads, writes)
        return tok

    def dma(self, queue, fn, semkey, reads=(), writes=(), deps=()):
        prev = self.dma_cnt.get(semkey, 0)
        toks = self._deps(reads, writes, deps)
        if prev:
            if toks.get(semkey, 0) < prev:
                toks[semkey] = prev
        self.dma_cnt[semkey] = prev + 16
        tok = (semkey, prev + 16)
        self.ops[queue].append((fn, toks, (semkey, 16)))
        self._commit(tok, reads, writes)
        return tok

    def fence(self):
        f = dict(self.cnt)
        f.update(self.dma_cnt)
        return {k: v for k, v in f.items() if v > 0}


def build_program():
    nc = bass.Bass("TRN2", target_bir_lowering=False)
    xT = nc.dram_tensor("xT", [NT, D, TW], F32, kind="ExternalInput").ap()
    w_in = nc.dram_tensor("w_in", [D, EIN], F32, kind="ExternalInput").ap()
    w_out = nc.dram_tensor("w_out", [D, D], F32, kind="ExternalInput").ap()
    w_up = nc.dram_tensor("w_up", [D, DFF], F32, kind="ExternalInput").ap()
    w_down = nc.dram_tensor("w_down", [DFF, D], F32, kind="ExternalInput").ap()
    cst = nc.dram_tensor("cst", [128, NCF], F32, kind="ExternalInput").ap()
    outT = nc.dram_tensor("outT", [D, TPC], F32, kind="ExternalOutput").ap()

    w_in_v = w_in.rearrange("(kq kk p) e -> p kq kk e", p=128, kk=8)
    w_out_v = w_out.rearrange("(kq kk p) e -> p kq kk e", p=128, kk=8)
    w_up_v = w_up.rearrange("(kq kk p) e -> p kq kk e", p=128, kk=8)
    w_dn_v = w_down.rearrange("(g f p) e -> p g f e", p=128, f=4)
    xT_v = xT.rearrange("t (c p) n -> t p c n", p=128)
    outT_v = outT.rearrange("(c p) n -> p c n", p=128)

    es = ExitStack()
    with es:
        CONSTF = es.enter_context(nc.sbuf_tensor("constf", [128, NCF], F32))
        ONES = es.enter_context(nc.sbuf_tensor("ones", [128, 128], BF16))
        RING = es.enter_context(nc.sbuf_tensor("ring", [128, NR, 4096], BF16))
        HT = es.enter_context(nc.sbuf_tensor("ht", [128, KC, T], F32))
        R1 = es.enter_context(nc.sbuf_tensor("r1", [128, KC, T], BF16))
        R2 = es.enter_context(nc.sbuf_tensor("r2", [128, KC * T], F32))
        HID = es.enter_context(nc.sbuf_tensor("hid", [128, 2, 4, T], BF16))
        TMPF = es.enter_context(nc.sbuf_tensor("tmpf", [128, 2, T], F32))
        RSTD = es.enter_context(nc.sbuf_tensor("rstd", [128, 2, TW], F32))
        SQS = es.enter_context(nc.sbuf_tensor("sqs", [128, 4, TW], BF16))
        PS = es.enter_context(nc.psum_tensor("ps", [128, 8, 512], F32))

        sems = {}
        for e in ENGS:
            sems[e] = es.enter_context(nc.semaphore("s_" + e))

        def dsem(key):
            if key not in sems:
                sems[key] = es.enter_context(nc.semaphore("d_" + key))
            return key

        sch = Sched()

        R2b = R2.bitcast(BF16)
        UT = R2b[:, 0:KC * TW].rearrange("p (c n) -> p c n", c=KC)
        XS0 = KC * TW // 2
        XS = R2[:, XS0:XS0 + 3 * TW].rearrange("p (a n) -> p a n", a=3)
        TB0 = XS0 + 3 * TW
        ACC = R2[:, :].rearrange("p (c n) -> p c n", c=KC)
        HTflat = HT[:, :, :].rearrange("p c n -> p (c n)")
        TBASE = 16 * T

        def tmp_view(i, which):
            if which < 3:
                off = TBASE + i * (3 * TW) + which * TW
                return HTflat[:, off:off + TW]
            off = TB0 + i * TW
            return R2[:, off:off + TW]

        def cb_view(i):
            off = TB0 + 4 * TW + i * 256
            return R2[:, off:off + 256].bitcast(BF16)

        def sq_view(i):
            off = TB0 + 4 * TW + 4 * 256 + i * 256
            return R2[:, off:off + 256].bitcast(BF16)

        cg = lambda n, c: CONSTF[:, C_G + n * 32 + c:C_G + n * 32 + c + 1]
        dww = lambda j, k: CONSTF[:, C_DWW + j * 31 + k:C_DWW + j * 31 + k + 1]
        dwb = lambda j: CONSTF[:, C_DWB + j:C_DWB + j + 1]
        lng = lambda j: CONSTF[:, C_LNG + j:C_LNG + j + 1]
        lnb = lambda j: CONSTF[:, C_LNB + j:C_LNB + j + 1]
        scw = lambda j, k: CONSTF[:, C_SCW + j * 3 + k:C_SCW + j * 3 + k + 1]
        eps_rms = CONSTF[:, C_EPS:C_EPS + 1]
        eps_ln = CONSTF[:, C_EPS + 1:C_EPS + 2]

        b_const = Buf("const")
        b_ones = Buf("ones")
        b_ring = [Buf("ring%d" % i) for i in range(NR)]
        b_ps = [Buf("ps%d" % i) for i in range(8)]
        b_halo = [b_ps[4]] * 4
        b_ht = [Buf("ht%d" % i) for i in range(KC)]
        b_r1 = [Buf("r1_%d" % i) for i in range(KC)]
        b_ut = [Buf("ut%d" % i) for i in range(KC)]
        b_acc = [Buf("acc%d" % i) for i in range(KC)]
        b_xs = [Buf("xs%d" % i) for i in range(3)]
        b_hid = [[Buf("hid%d_%d" % (i, f)) for f in range(4)] for i in range(2)]
        b_tmpf = [Buf("tmpf%d" % i) for i in range(2)]
        b_rstd = [Buf("rstd%d" % i) for i in range(2)]
        b_sqs = [Buf("sqs%d" % i) for i in range(4)]
        b_t = [[Buf("t%d_%d" % (i, w)) for w in range(4)] for i in range(4)]
        b_cb = [Buf("cb%d" % i) for i in range(4)]
        b_sq4 = [Buf("sq4_%d" % i) for i in range(4)]

        ring_sem = [dsem("ring%d" % i) for i in range(NR)]
        xs_sem = [dsem("xs%d" % i) for i in range(3)]
        out_sem = [dsem("out%d" % i) for i in range(4)]
        cst_sem = dsem("cst")
        NXP = 16
        xp_sem = [dsem("xp%d" % i) for i in range(NXP // 4)]
        b_xp = [Buf("xp%d" % i) for i in range(NXP)]

        state = {"unit": 0, "sq": 0, "xs": 0, "out": 0, "pool_skip": 0}
        pool_pending = []
        POOL_PER_UNIT = 5
        POOL_TAPS = []
        DVE_TAPS = list(range(1, 31))
        b_acc2 = [Buf("acc2_%d" % i) for i in range(4)]

        def acc2_view(i):
            if i < 3:
                off = TB0 + 4 * TW + 8 * 256 + i * T
                return R2[:, off:off + T]
            off = TBASE + 4 * 3 * TW
            return HTflat[:, off:off + T]


        def load_unit(src_ap, shape3):
            u = state["unit"]
            state["unit"] += 1
            s = u % NR
            a, c = shape3
            dst = RING[:, s, :].rearrange("p (a c) -> p a c", a=a)
            sch.dma("pool", lambda e, dst=dst, src=src_ap: e.dma_start(out=dst, in_=src),
                    ring_sem[s], writes=[b_ring[s]])
            if state["pool_skip"] > 0:
                state["pool_skip"] -= 1
            else:
                n = 0
                while pool_pending and n < POOL_PER_UNIT:
                    pool_pending.pop(0)()
                    n += 1
            return s, dst

        def flush_pool():
            while pool_pending:
                pool_pending.pop(0)()

        def mm(out, lhsT, rhs, start, stop, reads, writes, signal, skip=False):
            if skip:
                fn = lambda e: e.matmul(out, lhsT, rhs, start=start, stop=stop, skip_group_check=True)
            else:
                fn = lambda e: e.matmul(out, lhsT, rhs, start=start, stop=stop)
            return sch.op("pe", fn, reads=reads, writes=writes, signal=signal)

        def act(out, in_, func, reads, writes, bias=None, scale=None, deps=()):
            kw = {}
            if bias is not None:
                kw["bias"] = bias
            if scale is not None:
                kw["scale"] = scale
            return sch.op("act", lambda e: e.activation(out, in_, func, **kw), reads=reads, writes=writes, deps=deps)

        def dve(fn, reads, writes, deps=()):
            return sch.op("dve", fn, reads=reads, writes=writes, deps=deps)

        def next_sq():
            i = state["sq"] % 4
            state["sq"] += 1
            return i

        def stats_rstd(bank_ap_list, n_feat, eps_ap, rstd_idx, bufs_read):
            for ps_ap, sl, pb in bank_ap_list:
                o = RSTD[:, rstd_idx, sl]
                act(o, ps_ap, AF.Sqrt, reads=[pb, b_const], writes=[b_rstd[rstd_idx]],
                    bias=eps_ap, scale=1.0 / n_feat)
            o = RSTD[:, rstd_idx, :] if len(bank_ap_list) == 2 else RSTD[:, rstd_idx, bank_ap_list[0][1]]
            dve(lambda e, o=o: e.reciprocal(out=o, in_=o), reads=[b_rstd[rstd_idx]], writes=[b_rstd[rstd_idx]])

        sch.dma("sp", lambda e: e.dma_start(out=CONSTF[:, :], in_=cst), cst_sem, writes=[b_const])
        dve(lambda e: e.memset(ONES[:, :], 1.0), reads=[], writes=[b_ones])

        TMPFflat = TMPF[:, :, :].rearrange("p a n -> p (a n)")
        XGROUPS = [(30, 1), (31, 1)] + [(c0, 3) for c0 in range(0, 30, 3)]
        xg_sem = [dsem("xg%d" % i) for i in range(len(XGROUPS))]

        b_xst = [Buf("xst%d" % i) for i in range(KC)]

        def x_region(c0, n):
            if c0 == 30:
                return RSTD[:, 1, :], [b_rstd[1]]
            if c0 == 31:
                return TMPFflat[:, 0:TW], [b_tmpf[0], b_tmpf[1]]
            lo, hi = c0 * TW, (c0 + n) * TW
            return HTflat[:, lo:hi], [b_xst[c] for c in range(c0, c0 + n)]

        def x_ht_bufs(c):
            if c >= 30:
                return []
            lo, hi = c * TW, (c + 1) * TW
            return [b_ht[i] for i in range(lo // T, (hi - 1) // T + 1)]

        def make_pass1(t, tf):
            st = {"gi": 0}

            def need_ht(c0, n):
                return -1 if c0 >= 30 else ((c0 + n) * TW - 1) // T

            def advance(limit):
                while st["gi"] < len(XGROUPS):
                    gi = st["gi"]
                    c0, n = XGROUPS[gi]
                    if need_ht(c0, n) > limit:
                        break
                    reg, bufs = x_region(c0, n)
                    dst = reg.rearrange("p (a n) -> p a n", a=n)
                    src = xT_v[t, :, c0:c0 + n, :]
                    sch.dma("sp", lambda e, dst=dst, src=src: e.dma_start(out=dst, in_=src),
                            xg_sem[gi], writes=bufs, deps=[tf])
                    for a_ in range(n):
                        c = c0 + a_
                        xv, xb = x_region(c, 1)
                        si = next_sq()
                        if c % 2 == 0:
                            act(SQS[:, si, :], xv, AF.Square, reads=xb + x_ht_bufs(c), writes=[b_sqs[si]], deps=[tf])
                        else:
                            dve(lambda e, xv=xv, si=si: e.tensor_tensor(out=SQS[:, si, :], in0=xv, in1=xv, op=ALU.mult),
                                reads=xb + x_ht_bufs(c), writes=[b_sqs[si]], deps=[tf])
                        first = (gi == 0 and a_ == 0)
                        last = (gi == len(XGROUPS) - 1 and a_ == n - 1)
                        mm(PS[:, 6, :], ONES[:, :], SQS[:, si, HALO:TW], first, last,
                           reads=[b_sqs[si], b_ones], writes=[b_ps[6]], signal=False)
                        mm(PS[:, 4, 0:HALO], ONES[:, :], SQS[:, si, 0:HALO], first, last,
                           reads=[b_sqs[si], b_ones], writes=[b_halo[0]], signal=True)
                    st["gi"] += 1
            return advance

        store_tok = {}
        tile_fence = {}
        for t in range(NT):
            tf = dict(tile_fence)
            make_pass1(t, tf)(KC)
            stats_rstd([(PS[:, 6, :], slice(HALO, TW), b_ps[6]), (PS[:, 4, 0:HALO], slice(0, HALO), b_halo[0])],
                       D, eps_rms, 0, None)
            for c in range(KC):
                xv, xb = x_region(c, 1)
                lo, hi = c * (TW // 2), (c + 1) * (TW // 2)
                deps = [store_tok[a_] for a_ in range(lo // T, (hi - 1) // T + 1) if a_ in store_tok]
                if c == KC - 1:
                    deps = list(store_tok.values())
                dve(lambda e, xv=xv, c=c: e.scalar_tensor_tensor(
                        out=UT[:, c, :], in0=xv, scalar=cg(0, c), in1=RSTD[:, 0, :],
                        op0=ALU.mult, op1=ALU.mult),
                    reads=xb + x_ht_bufs(c) + [b_rstd[0], b_const], writes=[b_ut[c]], deps=deps + [tf])
            store_tok = {}

            pending_stats = []

            def emit_stats(lst):
                for h in lst:
                    i = h % 4
                    mm(PS[:, 6, :], ONES[:, :], cb_view(i), h == 0, h == 15,
                       reads=[b_cb[i], b_ones], writes=[b_ps[6]], signal=False)
                    mm(PS[:, 7, :], ONES[:, :], sq_view(i), h == 0, h == 15,
                       reads=[b_sq4[i], b_ones], writes=[b_ps[7]], signal=True)

            halo_rot = 0
            for q in range(4):
                for kind in ("V", "G", "C", "X", "B"):
                    e0 = {"V": 0, "G": 16, "B": 32, "C": 48, "X": 64}[kind] + 4 * q
                    has_halo = kind != "B"
                    hr = 0
                    if has_halo:
                        hr = halo_rot % 4
                        halo_rot += 1
                    for kq in range(4):
                        s, wv = load_unit(w_in_v[:, kq, :, e0 * 128:e0 * 128 + 512], (8, 512))
                        for i in range(4):
                            for kk in range(8):
                                k = kq * 8 + kk
                                last = (kk == 7) and (not has_halo)
                                mm(PS[:, i, :], wv[:, kk, i * 128:(i + 1) * 128], UT[:, k, HALO:TW],
                                   k == 0, k == KC - 1, reads=[b_ring[s], b_ut[k]], writes=[b_ps[i]],
                                   signal=last)
                            if has_halo:
                                for kk in range(8):
                                    k = kq * 8 + kk
                                    mm(PS[:, 4, hr * 128 + i * 32:hr * 128 + (i + 1) * 32],
                                       wv[:, kk, i * 128:(i + 1) * 128], UT[:, k, 0:HALO],
                                       (k == 0 and i == 0), k == KC - 1,
                                       reads=[b_ring[s], b_ut[k]], writes=[b_halo[hr]],
                                       signal=(kk == 7), skip=True)
                    if kind == "G" and pending_stats:
                        emit_stats(pending_stats)
                        pending_stats = []
                    for i in range(4):
                        h = 4 * q + i
                        main = PS[:, i, :]
                        halo = PS[:, 4, hr * 128 + i * 32:hr * 128 + (i + 1) * 32]
                        if kind in ("V", "G", "C", "X"):
                            w = {"V": 0, "G": 1, "C": 2, "X": 3}[kind]
                            fnc = AF.Sigmoid if kind == "G" else AF.Copy
                            tv = tmp_view(i, w)
                            act(tv[:, HALO:TW], main, fnc, reads=[b_ps[i]], writes=[b_t[i][w]])
                            act(tv[:, 0:HALO], halo, fnc, reads=[b_halo[hr]], writes=[b_t[i][w]])
                        else:
                            y = tmp_view(i, 1)[:, 0:T]
                            dve(lambda e, y=y, main=main, h=h: e.tensor_tensor(
                                    out=R1[:, 16 + h, :], in0=main, in1=y, op=ALU.mult),
                                reads=[b_ps[i], b_t[i][1]], writes=[b_r1[16 + h]])
                    if kind == "G":
                        for i in range(4):
                            t1 = tmp_view(i, 0)
                            t2 = tmp_view(i, 1)
                            dve(lambda e, t1=t1, t2=t2: e.tensor_tensor(out=t1, in0=t1, in1=t2, op=ALU.mult),
                                reads=[b_t[i][0], b_t[i][1]], writes=[b_t[i][0]])
                        for i in range(4):
                            h = 4 * q + i
                            t1 = tmp_view(i, 0)
                            pre = HT[:, h, :]
                            dve(lambda e, t1=t1, pre=pre, h=h: e.tensor_scalar(
                                    out=pre, in0=t1[:, 2:2 + T], scalar1=dww(h, 0), scalar2=dwb(h),
                                    op0=ALU.mult, op1=ALU.add),
                                reads=[b_t[i][0], b_const], writes=[b_ht[h]])
                        for k in DVE_TAPS:
                            for i in range(4):
                                h = 4 * q + i
                                t1 = tmp_view(i, 0)
                                pre = HT[:, h, :]
                                dve(lambda e, t1=t1, pre=pre, h=h, k=k: e.scalar_tensor_tensor(
                                        out=pre, in0=t1[:, 2 + k:2 + k + T], scalar=dww(h, k), in1=pre,
                                        op0=ALU.mult, op1=ALU.add),
                                    reads=[b_t[i][0], b_const], writes=[b_ht[h]])
                        for n_, k in enumerate(POOL_TAPS):
                            for i in range(4):
                                h = 4 * q + i
                                t1 = tmp_view(i, 0)
                                a2 = acc2_view(i)
                                if n_ == 0:
                                    def reg(i=i, h=h, t1=t1, a2=a2, k=k):
                                        sch.op("pool", lambda e: e.tensor_scalar(
                                                   out=a2, in0=t1[:, 2 + k:2 + k + T], scalar1=dww(h, k),
                                                   scalar2=None, op0=ALU.mult),
                                               reads=[b_t[i][0], b_const], writes=[b_acc2[i]])
                                else:
                                    def reg(i=i, h=h, t1=t1, a2=a2, k=k):
                                        sch.op("pool", lambda e: e.scalar_tensor_tensor(
                                                   out=a2, in0=t1[:, 2 + k:2 + k + T], scalar=dww(h, k), in1=a2,
                                                   op0=ALU.mult, op1=ALU.add),
                                               reads=[b_t[i][0], b_const], writes=[b_acc2[i]])
                                pool_pending.append(reg)
                        state["pool_skip"] = 3
                    if kind == "X":
                        for i in range(4):
                            t3 = tmp_view(i, 2)
                            t4 = tmp_view(i, 3)
                            dve(lambda e, t3=t3, t4=t4: e.tensor_tensor(out=t3, in0=t3, in1=t4, op=ALU.mult),
                                reads=[b_t[i][2], b_t[i][3]], writes=[b_t[i][2]])
                        for i in range(4):
                            h = 4 * q + i
                            t3 = tmp_view(i, 2)
                            y = tmp_view(i, 1)[:, 0:T]
                            dve(lambda e, t3=t3, y=y, h=h: e.tensor_scalar(
                                    out=y, in0=t3[:, 30:30 + T], scalar1=scw(h, 0), scalar2=None, op0=ALU.mult),
                                reads=[b_t[i][2], b_const], writes=[b_t[i][1]])
                        for k in (1, 2):
                            for i in range(4):
                                h = 4 * q + i
                                t3 = tmp_view(i, 2)
                                y = tmp_view(i, 1)[:, 0:T]
                                dve(lambda e, t3=t3, y=y, h=h, k=k: e.scalar_tensor_tensor(
                                        out=y, in0=t3[:, 30 + k:30 + k + T], scalar=scw(h, k), in1=y,
                                        op0=ALU.mult, op1=ALU.add),
                                    reads=[b_t[i][2], b_const], writes=[b_t[i][1]])
                    if kind == "B":
                        flush_pool()
                        if POOL_TAPS:
                            for i in range(4):
                                h = 4 * q + i
                                pre = HT[:, h, :]
                                a2 = acc2_view(i)
                                dve(lambda e, pre=pre, a2=a2: e.tensor_tensor(out=pre, in0=pre, in1=a2, op=ALU.add),
                                    reads=[b_ht[h], b_acc2[i]], writes=[b_ht[h]])
                        for i in range(4):
                            h = 4 * q + i
                            pre = HT[:, h, :]
                            act(cb_view(i), pre, AF.Copy, reads=[b_ht[h]], writes=[b_cb[i]])
                            act(sq_view(i), pre, AF.Square, reads=[b_ht[h]], writes=[b_sq4[i]])
                            pending_stats.append(h)
            emit_stats(pending_stats)
            pending_stats = []

            dve(lambda e: e.tensor_scalar(out=TMPF[:, 0, :], in0=PS[:, 6, :], scalar1=1.0 / 2048, scalar2=None,
                                          op0=ALU.mult),
                reads=[b_ps[6]], writes=[b_tmpf[0]])
            dve(lambda e: e.tensor_tensor(out=TMPF[:, 1, :], in0=TMPF[:, 0, :], in1=TMPF[:, 0, :], op=ALU.mult),
                reads=[b_tmpf[0]], writes=[b_tmpf[1]])
            dve(lambda e: e.scalar_tensor_tensor(out=TMPF[:, 1, :], in0=PS[:, 7, :], scalar=1.0 / 2048,
                                                 in1=TMPF[:, 1, :], op0=ALU.mult, op1=ALU.subtract),
                reads=[b_ps[7], b_tmpf[1]], writes=[b_tmpf[1]])
            act(RSTD[:, 1, 0:T], TMPF[:, 1, :], AF.Sqrt, reads=[b_tmpf[1], b_const], writes=[b_rstd[1]],
                bias=eps_ln, scale=1.0)
            dve(lambda e: e.reciprocal(out=RSTD[:, 1, 0:T], in_=RSTD[:, 1, 0:T]),
                reads=[b_rstd[1]], writes=[b_rstd[1]])
            for h0 in range(0, 16, 4):
                for h in range(h0, h0 + 4):
                    pre = HT[:, h, :]
                    dve(lambda e, pre=pre: e.tensor_tensor(out=pre, in0=pre, in1=TMPF[:, 0, :], op=ALU.subtract),
                        reads=[b_ht[h], b_tmpf[0]], writes=[b_ht[h]])
                for h in range(h0, h0 + 4):
                    pre = HT[:, h, :]
                    dve(lambda e, pre=pre: e.tensor_tensor(out=pre, in0=pre, in1=RSTD[:, 1, 0:T], op=ALU.mult),
                        reads=[b_ht[h], b_rstd[1]], writes=[b_ht[h]])
                for h in range(h0, h0 + 4):
                    pre = HT[:, h, :]
                    act(R1[:, h, :], pre, AF.Silu, reads=[b_ht[h], b_const], writes=[b_r1[h]],
                        bias=lnb(h), scale=lng(h))

            for g_ in range(NXP // 4):
                lo_, hi_ = g_ * 4 * T, (g_ + 1) * 4 * T
                utb = [b_ut[c] for c in range(lo_ // (TW // 2), min(KC - 1, (hi_ - 1) // (TW // 2)) + 1)]
                dstp = R2[:, lo_:hi_].rearrange("p (a n) -> p a n", a=4)
                srcp = xT_v[t, :, 4 * g_:4 * g_ + 4, HALO:TW]
                sch.dma("sp", lambda e, dstp=dstp, srcp=srcp: e.dma_start(out=dstp, in_=srcp),
                        xp_sem[g_], writes=[b_xp[c] for c in range(4 * g_, 4 * g_ + 4)] + utb)
            pend = []

            def emit_post_stats(lst):
                for j in lst:
                    i = j % 4
                    mm(PS[:, 7, :], ONES[:, :], sq_view(i), j == 0, j == KC - 1,
                       reads=[b_sq4[i], b_ones], writes=[b_ps[7]], signal=True)

            for jg in range(8):
                for kqi, kq in enumerate((2, 3, 0, 1)):
                    s, wv = load_unit(w_out_v[:, kq, :, jg * 512:(jg + 1) * 512], (8, 512))
                    for i in range(4):
                        for kk in range(8):
                            k = kq * 8 + kk
                            mm(PS[:, i, :], wv[:, kk, i * 128:(i + 1) * 128], R1[:, k, :],
                               (kqi == 0 and kk == 0), (kqi == 3 and kk == 7),
                               reads=[b_ring[s], b_r1[k]], writes=[b_ps[i]], signal=(kk == 7))
                emit_post_stats(pend)
                pend = []
                for i in range(4):
                    j = jg * 4 + i
                    act(HT[:, j, :], PS[:, i, :], AF.Copy, reads=[b_ps[i]], writes=[b_ht[j]])
                    act(sq_view(i), PS[:, i, :], AF.Square, reads=[b_ps[i]], writes=[b_sq4[i]])
                    pend.append(j)
            emit_post_stats(pend)
            pend = []
            stats_rstd([(PS[:, 7, :], slice(0, T), b_ps[7])], D, eps_rms, 0, None)
            for c0 in range(0, KC, 2):
                xis = []
                for c in (c0, c0 + 1):
                    if c < NXP:
                        xis.append((R2[:, c * T:(c + 1) * T], b_xp[c]))
                        continue
                    xi = state["xs"] % 3
                    state["xs"] += 1
                    xis.append((XS[:, xi, 0:T], b_xs[xi]))
                    src = xT_v[t, :, c, HALO:TW]
                    sch.dma("sp", lambda e, xi=xi, src=src: e.dma_start(out=XS[:, xi, 0:T], in_=src),
                            xs_sem[xi], writes=[b_xs[xi]])
                for c in (c0, c0 + 1):
                    hc = HT[:, c, :]
                    dve(lambda e, hc=hc: e.tensor_tensor(out=hc, in0=hc, in1=RSTD[:, 0, 0:T], op=ALU.mult),
                        reads=[b_ht[c], b_rstd[0]], writes=[b_ht[c]])
                for c, (xv_, xb_) in zip((c0, c0 + 1), xis):
                    hc = HT[:, c, :]
                    dve(lambda e, hc=hc, xv_=xv_, c=c: e.scalar_tensor_tensor(
                            out=hc, in0=hc, scalar=cg(1, c), in1=xv_, op0=ALU.mult, op1=ALU.add),
                        reads=[b_ht[c], xb_, b_const], writes=[b_ht[c]])
                for c in (c0, c0 + 1):
                    hc = HT[:, c, :]
                    si = next_sq()
                    act(SQS[:, si, 0:T], hc, AF.Square, reads=[b_ht[c]], writes=[b_sqs[si]])
                    mm(PS[:, 6, :], ONES[:, :], SQS[:, si, 0:T], c == 0, c == KC - 1,
                       reads=[b_sqs[si], b_ones], writes=[b_ps[6]], signal=True)
            stats_rstd([(PS[:, 6, :], slice(0, T), b_ps[6])], D, eps_rms, 1, None)
            for c in range(KC):
                dve(lambda e, c=c: e.scalar_tensor_tensor(
                        out=R1[:, c, :], in0=HT[:, c, :], scalar=cg(2, c), in1=RSTD[:, 1, 0:T],
                        op0=ALU.mult, op1=ALU.mult),
                    reads=[b_ht[c], b_rstd[1], b_const], writes=[b_r1[c]])

            NG = DFF // 512
            fence_c = sch.fence()

            def up_unit(g, kq):
                s, wv = load_unit(w_up_v[:, kq, :, g * 512:(g + 1) * 512], (8, 512))
                for f in range(4):
                    for kk in range(8):
                        k = kq * 8 + kk
                        mm(PS[:, f, :], wv[:, kk, f * 128:(f + 1) * 128], R1[:, k, :],
                           k == 0, k == KC - 1, reads=[b_ring[s], b_r1[k]], writes=[b_ps[f]],
                           signal=(kk == 7))

            def up_evac(g):
                hb = g % 2
                for f in range(4):
                    ti = f % 2
                    act(TMPF[:, ti, :], PS[:, f, :], AF.Relu, reads=[b_ps[f]], writes=[b_tmpf[ti]])
                    act(HID[:, hb, f, :], TMPF[:, ti, :], AF.Square, reads=[b_tmpf[ti]], writes=[b_hid[hb][f]])

            dn_rot = [0]
            final_stats = []

            def flush_final():
                for (j, si) in final_stats:
                    mm(PS[:, 7, :], ONES[:, :], SQS[:, si, 0:T], j == 0, j == KC - 1,
                       reads=[b_sqs[si], b_ones], writes=[b_ps[7]], signal=True)
                final_stats.clear()

            def down_unit(g, jb):
                hb = g % 2
                s, wv = load_unit(w_dn_v[:, g, :, jb * 1024:(jb + 1) * 1024], (4, 1024))
                for jj in range(8):
                    j = jb * 8 + jj
                    bank = 4 + dn_rot[0] % 3
                    dn_rot[0] += 1
                    for f in range(4):
                        mm(PS[:, bank, :], wv[:, f, jj * 128:(jj + 1) * 128], HID[:, hb, f, :],
                           f == 0, f == 3, reads=[b_ring[s], b_hid[hb][f]], writes=[b_ps[bank]],
                           signal=(f == 3))
                    if g == 0:
                        dve(lambda e, j=j, bank=bank: e.tensor_copy(out=ACC[:, j, :], in_=PS[:, bank, :]),
                            reads=[b_ps[bank]], writes=[b_acc[j]], deps=[fence_c])
                    else:
                        dve(lambda e, j=j, bank=bank: e.tensor_tensor(
                                out=ACC[:, j, :], in0=ACC[:, j, :], in1=PS[:, bank, :], op=ALU.add),
                            reads=[b_ps[bank], b_acc[j]], writes=[b_acc[j]])
                    if g == NG - 1:
                        si = next_sq()
                        act(SQS[:, si, 0:T], ACC[:, j, :], AF.Square, reads=[b_acc[j]], writes=[b_sqs[si]])
                        final_stats.append((j, si))
                        if len(final_stats) == 2:
                            flush_final()

            for kq in range(4):
                up_unit(0, kq)
            up_evac(0)
            for g in range(1, NG):
                for u in range(4):
                    up_unit(g, u)
                    down_unit(g - 1, u)
                up_evac(g)
            for jb in range(4):
                down_unit(NG - 1, jb)
            flush_final()

            stats_rstd([(PS[:, 7, :], slice(0, T), b_ps[7])], D, eps_rms, 0, None)
            for j0 in range(0, KC, 2):
                for j in (j0, j0 + 1):
                    aj = ACC[:, j, :]
                    dve(lambda e, aj=aj: e.tensor_tensor(out=aj, in0=aj, in1=RSTD[:, 0, 0:T], op=ALU.mult),
                        reads=[b_acc[j], b_rstd[0]], writes=[b_acc[j]])
                for j in (j0, j0 + 1):
                    aj = ACC[:, j, :]
                    dve(lambda e, aj=aj, j=j: e.scalar_tensor_tensor(
                            out=aj, in0=aj, scalar=cg(3, j), in1=HT[:, j, :], op0=ALU.mult, op1=ALU.add),
                        reads=[b_acc[j], b_ht[j], b_const], writes=[b_acc[j]])
                for j in (j0, j0 + 1):
                    aj = ACC[:, j, :]
                    oi = state["out"] % 4
                    state["out"] += 1
                    dst = outT_v[:, j, t * T:(t + 1) * T]
                    store_tok[j] = sch.dma("sp", lambda e, aj=aj, dst=dst: e.dma_start(out=dst, in_=aj),
                                           out_sem[oi], reads=[b_acc[j]])
            tile_fence = sch.fence()

        total_wait = sch.fence()
        with nc.Block() as block:
            def make_body(eng):
                def body(e):
                    known = {}
                    for fn, toks, sig in sch.ops[eng]:
                        for k, v in toks.items():
                            if known.get(k, 0) >= v:
                                continue
                            e.wait_ge(sems[k], v)
                            known[k] = v
                        inst = fn(e)
                        if sig is not None:
                            inst.then_inc(sems[sig[0]], sig[1])
                    if eng == "sp":
                        for k in out_sem:
                            if sch.dma_cnt.get(k, 0):
                                e.wait_ge(sems[k], sch.dma_cnt[k])
                return body
            block.tensor(make_body("pe"))
            block.scalar(make_body("act"))
            block.vector(make_body("dve"))
            block.gpsimd(make_body("pool"))
            block.sync(make_body("sp"))
    return nc


def _prep_inputs(x, norm_mix_pre, w_in, conf_dw_w, conf_dw_b, conf_ln_g, conf_ln_b,
                 sc_conv_w, w_out, norm_mix_post, norm_mlp_pre, w_up, w_down, norm_mlp_post):
    f = np.float32
    x2 = np.asarray(x, dtype=f).reshape(S, D)
    xpad = np.concatenate([np.zeros((HALO, D), f), x2], axis=0)
    cst = np.zeros((128, NCF), f)
    for n, g in enumerate([norm_mix_pre, norm_mix_post, norm_mlp_pre, norm_mlp_post]):
        cst[:, C_G + n * 32:C_G + (n + 1) * 32] = np.asarray(g, f).reshape(32, 128).T
    cst[:, C_DWW:C_DWW + 496] = np.asarray(conf_dw_w, f).reshape(31, 16, 128).transpose(2, 1, 0).reshape(128, 496)
    cst[:, C_DWB:C_DWB + 16] = np.asarray(conf_dw_b, f).reshape(16, 128).T
    cst[:, C_LNG:C_LNG + 16] = np.asarray(conf_ln_g, f).reshape(16, 128).T
    cst[:, C_LNB:C_LNB + 16] = np.asarray(conf_ln_b, f).reshape(16, 128).T
    cst[:, C_SCW:C_SCW + 48] = np.asarray(sc_conv_w, f).reshape(3, 16, 128).transpose(2, 1, 0).reshape(128, 48)
    cst[:, C_EPS] = RMS_EPS
    cst[:, C_EPS + 1] = LN_EPS
    w_in = np.ascontiguousarray(w_in, dtype=f)
    w_out = np.ascontiguousarray(w_out, dtype=f)
    w_up = np.ascontiguousarray(w_up, dtype=f)
    w_down = np.ascontiguousarray(w_down, dtype=f)
    in_maps = []
    for c in range(NCORES):
        xt = np.empty((NT, D, TW), f)
        for t in range(NT):
            s0 = c * TPC + t * T
            xt[t] = xpad[s0:s0 + TW, :].T
        in_maps.append({"xT": xt, "w_in": w_in, "w_out": w_out, "w_up": w_up, "w_down": w_down, "cst": cst})
    return in_maps


_NC_CACHE = {}


def kernel(**inputs):
    in_maps = _prep_inputs(**inputs)
    if "nc" not in _NC_CACHE:
        _NC_CACHE["nc"] = build_program()
    nc = _NC_CACHE["nc"]
    res = run_bass_kernel_spmd(nc, in_maps, core_ids=list(range(NCORES)))
    outs = [np.asarray(r["outT"], dtype=np.float32).T for r in res.results]
    return np.concatenate(outs, axis=0).reshape(1, S, D)
```

```python
import numpy as np
from contextlib import ExitStack

import concourse.bass as bass
import concourse.mybir as mybir
from concourse.bass_utils import run_bass_kernel_spmd

F32 = mybir.dt.float32
BF16 = mybir.dt.bfloat16
ALU = mybir.AluOpType
AF = mybir.ActivationFunctionType

NCORES = 8
D = 4096
S = 8192
TPC = S // NCORES
T = 512
NT = TPC // T
HALO = 32
TW = T + HALO
KC = D // 128
EIN = 10240
DFF = 16384
NR = 3
RMS_EPS = 1e-6
LN_EPS = 1e-5

C_G = 0
C_DWW = 128
C_DWB = 624
C_LNG = 640
C_LNB = 656
C_SCW = 672
C_EPS = 720
NCF = 768

ENGS = ["pe", "act", "dve", "pool", "sp"]


class Buf:
    __slots__ = ("w", "r", "name")

    def __init__(self, name=""):
        self.w = None
        self.r = {}
        self.name = name


class Sched:
    def __init__(self):
        self.ops = {e: [] for e in ENGS}
        self.cnt = {e: 0 for e in ENGS}
        self.dma_cnt = {}

    def _deps(self, reads, writes, deps):
        toks = {}

        def add(tok):
            if tok is None:
                return
            k, v = tok
            if toks.get(k, 0) < v:
                toks[k] = v
        for d in deps:
            if isinstance(d, dict):
                for k, v in d.items():
                    add((k, v))
            else:
                add(d)
        for b in reads:
            add(b.w)
        for b in writes:
            add(b.w)
            for k, v in b.r.items():
                add((k, v))
        return toks

    def _commit(self, tok, reads, writes):
        k, v = tok
        for b in reads:
            if b.r.get(k, 0) < v:
                b.r[k] = v
        for b in writes:
            b.w = tok
            b.r = {}

    def op(self, eng, fn, reads=(), writes=(), signal=True, deps=()):
        toks = self._deps(reads, writes, deps)
        if eng == "pe":
            toks.pop("pe", None)
        tok = (eng, self.cnt[eng] + 1)
        if signal:
            self.cnt[eng] += 1
        self.ops[eng].append((fn, toks, (eng, 1) if signal else None))
        self._commit(tok, reads, writes)
        return tok

    def dma(self, queue, fn, semkey, reads=(), writes=(), deps=()):
        prev = self.dma_cnt.get(semkey, 0)
        toks = self._deps(reads, writes, deps)
        if prev:
            if toks.get(semkey, 0) < prev:
                toks[semkey] = prev
        self.dma_cnt[semkey] = prev + 16
        tok = (semkey, prev + 16)
        self.ops[queue].append((fn, toks, (semkey, 16)))
        self._commit(tok, reads, writes)
        return tok

    def fence(self):
        f = dict(self.cnt)
        f.update(self.dma_cnt)
        return {k: v for k, v in f.items() if v > 0}


def build_program():
    nc = bass.Bass("TRN2", target_bir_lowering=False)
    xT = nc.dram_tensor("xT", [NT, D, TW], F32, kind="ExternalInput").ap()
    w_in = nc.dram_tensor("w_in", [D, EIN], F32, kind="ExternalInput").ap()
    w_out = nc.dram_tensor("w_out", [D, D], F32, kind="ExternalInput").ap()
    w_up = nc.dram_tensor("w_up", [D, DFF], F32, kind="ExternalInput").ap()
    w_down = nc.dram_tensor("w_down", [DFF, D], F32, kind="ExternalInput").ap()
    cst = nc.dram_tensor("cst", [128, NCF], F32, kind="ExternalInput").ap()
    outT = nc.dram_tensor("outT", [D, TPC], F32, kind="ExternalOutput").ap()

    w_in_v = w_in.rearrange("(kq kk p) e -> p kq kk e", p=128, kk=8)
    w_out_v = w_out.rearrange("(kq kk p) e -> p kq kk e", p=128, kk=8)
    w_up_v = w_up.rearrange("(kq kk p) e -> p kq kk e", p=128, kk=8)
    w_dn_v = w_down.rearrange("(g f p) e -> p g f e", p=128, f=4)
    xT_v = xT.rearrange("t (c p) n -> t p c n", p=128)
    outT_v = outT.rearrange("(c p) n -> p c n", p=128)

    es = ExitStack()
    with es:
        CONSTF = es.enter_context(nc.sbuf_tensor("constf", [128, NCF], F32))
        ONES = es.enter_context(nc.sbuf_tensor("ones", [128, 128], BF16))
        RING = es.enter_context(nc.sbuf_tensor("ring", [128, NR, 4096], BF16))
        HT = es.enter_context(nc.sbuf_tensor("ht", [128, KC, T], F32))
        R1 = es.enter_context(nc.sbuf_tensor("r1", [128, KC, T], BF16))
        R2 = es.enter_context(nc.sbuf_tensor("r2", [128, KC * T], F32))
        HID = es.enter_context(nc.sbuf_tensor("hid", [128, 2, 4, T], BF16))
        TMPF = es.enter_context(nc.sbuf_tensor("tmpf", [128, 2, T], F32))
        RSTD = es.enter_context(nc.sbuf_tensor("rstd", [128, 2, TW], F32))
        SQS = es.enter_context(nc.sbuf_tensor("sqs", [128, 4, TW], BF16))
        PS = es.enter_context(nc.psum_tensor("ps", [128, 8, 512], F32))

        sems = {}
        for e in ENGS:
            sems[e] = es.enter_context(nc.semaphore("s_" + e))

        def dsem(key):
            if key not in sems:
                sems[key] = es.enter_context(nc.semaphore("d_" + key))
            return key

        sch = Sched()

        R2b = R2.bitcast(BF16)
        UT = R2b[:, 0:KC * TW].rearrange("p (c n) -> p c n", c=KC)
        XS0 = KC * TW // 2
        XS = R2[:, XS0:XS0 + 3 * TW].rearrange("p (a n) -> p a n", a=3)
        TB0 = XS0 + 3 * TW
        ACC = R2[:, :].rearrange("p (c n) -> p c n", c=KC)
        HTflat = HT[:, :, :].rearrange("p c n -> p (c n)")
        TBASE = 16 * T

        def tmp_view(i, which):
            if which < 3:
                off = TBASE + i * (3 * TW) + which * TW
                return HTflat[:, off:off + TW]
            off = TB0 + i * TW
            return R2[:, off:off + TW]

        def cb_view(i):
            off = TB0 + 4 * TW + i * 256
            return R2[:, off:off + 256].bitcast(BF16)

        def sq_view(i):
            off = TB0 + 4 * TW + 4 * 256 + i * 256
            return R2[:, off:off + 256].bitcast(BF16)

        cg = lambda n, c: CONSTF[:, C_G + n * 32 + c:C_G + n * 32 + c + 1]
        dww = lambda j, k: CONSTF[:, C_DWW + j * 31 + k:C_DWW + j * 31 + k + 1]
        dwb = lambda j: CONSTF[:, C_DWB + j:C_DWB + j + 1]
        lng = lambda j: CONSTF[:, C_LNG + j:C_LNG + j + 1]
        lnb = lambda j: CONSTF[:, C_LNB + j:C_LNB + j + 1]
        scw = lambda j, k: CONSTF[:, C_SCW + j * 3 + k:C_SCW + j * 3 + k + 1]
        eps_rms = CONSTF[:, C_EPS:C_EPS + 1]
        eps_ln = CONSTF[:, C_EPS + 1:C_EPS + 2]

        b_const = Buf("const")
        b_ones = Buf("ones")
        b_ring = [Buf("ring%d" % i) for i in range(NR)]
        b_ps = [Buf("ps%d" % i) for i in range(8)]
        b_halo = [b_ps[4]] * 4
        b_ht = [Buf("ht%d" % i) for i in range(KC)]
        b_r1 = [Buf("r1_%d" % i) for i in range(KC)]
        b_ut = [Buf("ut%d" % i) for i in range(KC)]
        b_acc = [Buf("acc%d" % i) for i in range(KC)]
        b_xs = [Buf("xs%d" % i) for i in range(3)]
        b_hid = [[Buf("hid%d_%d" % (i, f)) for f in range(4)] for i in range(2)]
        b_tmpf = [Buf("tmpf%d" % i) for i in range(2)]
        b_rstd = [Buf("rstd%d" % i) for i in range(2)]
        b_sqs = [Buf("sqs%d" % i) for i in range(4)]
        b_t = [[Buf("t%d_%d" % (i, w)) for w in range(4)] for i in range(4)]
        b_cb = [Buf("cb%d" % i) for i in range(4)]
        b_sq4 = [Buf("sq4_%d" % i) for i in range(4)]

        ring_sem = [dsem("ring%d" % i) for i in range(NR)]
        xs_sem = [dsem("xs%d" % i) for i in range(3)]
        out_sem = [dsem("out%d" % i) for i in range(4)]
        cst_sem = dsem("cst")
        NXP = 16
        xp_sem = [dsem("xp%d" % i) for i in range(NXP // 4)]
        b_xp = [Buf("xp%d" % i) for i in range(NXP)]

        state = {"unit": 0, "sq": 0, "xs": 0, "out": 0, "pool_skip": 0}
        pool_pending = []
        POOL_PER_UNIT = 5
        POOL_TAPS = []
        DVE_TAPS = list(range(1, 31))
        b_acc2 = [Buf("acc2_%d" % i) for i in range(4)]

        def acc2_view(i):
            if i < 3:
                off = TB0 + 4 * TW + 8 * 256 + i * T
                return R2[:, off:off + T]
            off = TBASE + 4 * 3 * TW
            return HTflat[:, off:off + T]


        def load_unit(src_ap, shape3):
            u = state["unit"]
            state["unit"] += 1
            s = u % NR
            a, c = shape3
            dst = RING[:, s, :].rearrange("p (a c) -> p a c", a=a)
            sch.dma("pool", lambda e, dst=dst, src=src_ap: e.dma_start(out=dst, in_=src),
                    ring_sem[s], writes=[b_ring[s]])
            if state["pool_skip"] > 0:
                state["pool_skip"] -= 1
            else:
                n = 0
                while pool_pending and n < POOL_PER_UNIT:
                    pool_pending.pop(0)()
                    n += 1
            return s, dst

        def flush_pool():
            while pool_pending:
                pool_pending.pop(0)()

        def mm(out, lhsT, rhs, start, stop, reads, writes, signal, skip=False):
            if skip:
                fn = lambda e: e.matmul(out, lhsT, rhs, start=start, stop=stop, skip_group_check=True)
            else:
                fn = lambda e: e.matmul(out, lhsT, rhs, start=start, stop=stop)
            return sch.op("pe", fn, reads=reads, writes=writes, signal=signal)

        def act(out, in_, func, reads, writes, bias=None, scale=None, deps=()):
            kw = {}
            if bias is not None:
                kw["bias"] = bias
            if scale is not None:
                kw["scale"] = scale
            return sch.op("act", lambda e: e.activation(out, in_, func, **kw), reads=reads, writes=writes, deps=deps)

        def dve(fn, reads, writes, deps=()):
            return sch.op("dve", fn, reads=reads, writes=writes, deps=deps)

        def next_sq():
            i = state["sq"] % 4
            state["sq"] += 1
            return i

        def stats_rstd(bank_ap_list, n_feat, eps_ap, rstd_idx, bufs_read):
            for ps_ap, sl, pb in bank_ap_list:
                o = RSTD[:, rstd_idx, sl]
                act(o, ps_ap, AF.Sqrt, reads=[pb, b_const], writes=[b_rstd[rstd_idx]],
                    bias=eps_ap, scale=1.0 / n_feat)
            o = RSTD[:, rstd_idx, :] if len(bank_ap_list) == 2 else RSTD[:, rstd_idx, bank_ap_list[0][1]]
            dve(lambda e, o=o: e.reciprocal(out=o, in_=o), reads=[b_rstd[rstd_idx]], writes=[b_rstd[rstd_idx]])

        sch.dma("sp", lambda e: e.dma_start(out=CONSTF[:, :], in_=cst), cst_sem, writes=[b_const])
        dve(lambda e: e.memset(ONES[:, :], 1.0), reads=[], writes=[b_ones])

        TMPFflat = TMPF[:, :, :].rearrange("p a n -> p (a n)")
        XGROUPS = [(30, 1), (31, 1)] + [(c0, 3) for c0 in range(0, 30, 3)]
        xg_sem = [dsem("xg%d" % i) for i in range(len(XGROUPS))]

        b_xst = [Buf("xst%d" % i) for i in range(KC)]

        def x_region(c0, n):
            if c0 == 30:
                return RSTD[:, 1, :], [b_rstd[1]]
            if c0 == 31:
                return TMPFflat[:, 0:TW], [b_tmpf[0], b_tmpf[1]]
            lo, hi = c0 * TW, (c0 + n) * TW
            return HTflat[:, lo:hi], [b_xst[c] for c in range(c0, c0 + n)]

        def x_ht_bufs(c):
            if c >= 30:
                return []
            lo, hi = c * TW, (c + 1) * TW
            return [b_ht[i] for i in range(lo // T, (hi - 1) // T + 1)]

        def make_pass1(t, tf):
            st = {"gi": 0}

            def need_ht(c0, n):
                return -1 if c0 >= 30 else ((c0 + n) * TW - 1) // T

            def advance(limit):
                while st["gi"] < len(XGROUPS):
                    gi = st["gi"]
                    c0, n = XGROUPS[gi]
                    if need_ht(c0, n) > limit:
                        break
                    reg, bufs = x_region(c0, n)
                    dst = reg.rearrange("p (a n) -> p a n", a=n)
                    src = xT_v[t, :, c0:c0 + n, :]
                    sch.dma("sp", lambda e, dst=dst, src=src: e.dma_start(out=dst, in_=src),
                            xg_sem[gi], writes=bufs, deps=[tf])
                    for a_ in range(n):
                        c = c0 + a_
                        xv, xb = x_region(c, 1)
                        si = next_sq()
                        if c % 2 == 0:
                            act(SQS[:, si, :], xv, AF.Square, reads=xb + x_ht_bufs(c), writes=[b_sqs[si]], deps=[tf])
                        else:
                            dve(lambda e, xv=xv, si=si: e.tensor_tensor(out=SQS[:, si, :], in0=xv, in1=xv, op=ALU.mult),
                                reads=xb + x_ht_bufs(c), writes=[b_sqs[si]], deps=[tf])
                        first = (gi == 0 and a_ == 0)
                        last = (gi == len(XGROUPS) - 1 and a_ == n - 1)
                        mm(PS[:, 6, :], ONES[:, :], SQS[:, si, HALO:TW], first, last,
                           reads=[b_sqs[si], b_ones], writes=[b_ps[6]], signal=False)
                        mm(PS[:, 4, 0:HALO], ONES[:, :], SQS[:, si, 0:HALO], first, last,
                           reads=[b_sqs[si], b_ones], writes=[b_halo[0]], signal=True)
                    st["gi"] += 1
            return advance

        store_tok = {}
        tile_fence = {}
        for t in range(NT):
            tf = dict(tile_fence)
            make_pass1(t, tf)(KC)
            stats_rstd([(PS[:, 6, :], slice(HALO, TW), b_ps[6]), (PS[:, 4, 0:HALO], slice(0, HALO), b_halo[0])],
                       D, eps_rms, 0, None)
            for c in range(KC):
                xv, xb = x_region(c, 1)
                lo, hi = c * (TW // 2), (c + 1) * (TW // 2)
                deps = [store_tok[a_] for a_ in range(lo // T, (hi - 1) // T + 1) if a_ in store_tok]
                if c == KC - 1:
                    deps = list(store_tok.values())
                dve(lambda e, xv=xv, c=c: e.scalar_tensor_tensor(
                        out=UT[:, c, :], in0=xv, scalar=cg(0, c), in1=RSTD[:, 0, :],
                        op0=ALU.mult, op1=ALU.mult),
                    reads=xb + x_ht_bufs(c) + [b_rstd[0], b_const], writes=[b_ut[c]], deps=deps + [tf])
            store_tok = {}

            pending_stats = []

            def emit_stats(lst):
                for h in lst:
                    i = h % 4
                    mm(PS[:, 6, :], ONES[:, :], cb_view(i), h == 0, h == 15,
                       reads=[b_cb[i], b_ones], writes=[b_ps[6]], signal=False)
                    mm(PS[:, 7, :], ONES[:, :], sq_view(i), h == 0, h == 15,
                       reads=[b_sq4[i], b_ones], writes=[b_ps[7]], signal=True)

            halo_rot = 0
            for q in range(4):
                for kind in ("V", "G", "C", "X", "B"):
                    e0 = {"V": 0, "G": 16, "B": 32, "C": 48, "X": 64}[kind] + 4 * q
                    has_halo = kind != "B"
                    hr = 0
                    if has_halo:
                        hr = halo_rot % 4
                        halo_rot += 1
                    for kq in range(4):
                        s, wv = load_unit(w_in_v[:, kq, :, e0 * 128:e0 * 128 + 512], (8, 512))
                        for i in range(4):
                            for kk in range(8):
                                k = kq * 8 + kk
                                last = (kk == 7) and (not has_halo)
                                mm(PS[:, i, :], wv[:, kk, i * 128:(i + 1) * 128], UT[:, k, HALO:TW],
                                   k == 0, k == KC - 1, reads=[b_ring[s], b_ut[k]], writes=[b_ps[i]],
                                   signal=last)
                            if has_halo:
                                for kk in range(8):
                                    k = kq * 8 + kk
                                    mm(PS[:, 4, hr * 128 + i * 32:hr * 128 + (i + 1) * 32],
                                       wv[:, kk, i * 128:(i + 1) * 128], UT[:, k, 0:HALO],
                                       (k == 0 and i == 0), k == KC - 1,
                                       reads=[b_ring[s], b_ut[k]], writes=[b_halo[hr]],
                                       signal=(kk == 7), skip=True)
                    if kind == "G" and pending_stats:
                        emit_stats(pending_stats)
                        pending_stats = []
                    for i in range(4):
                        h = 4 * q + i
                        main = PS[:, i, :]
                        halo = PS[:, 4, hr * 128 + i * 32:hr * 128 + (i + 1) * 32]
                        if kind in ("V", "G", "C", "X"):
                            w = {"V": 0, "G": 1, "C": 2, "X": 3}[kind]
                            fnc = AF.Sigmoid if kind == "G" else AF.Copy
                            tv = tmp_view(i, w)
                            act(tv[:, HALO:TW], main, fnc, reads=[b_ps[i]], writes=[b_t[i][w]])
                            act(tv[:, 0:HALO], halo, fnc, reads=[b_halo[hr]], writes=[b_t[i][w]])
                        else:
                            y = tmp_view(i, 1)[:, 0:T]
                            dve(lambda e, y=y, main=main, h=h: e.tensor_tensor(
                                    out=R1[:, 16 + h, :], in0=main, in1=y, op=ALU.mult),
                                reads=[b_ps[i], b_t[i][1]], writes=[b_r1[16 + h]])
                    if kind == "G":
                        for i in range(4):
                            t1 = tmp_view(i, 0)
                            t2 = tmp_view(i, 1)
                            dve(lambda e, t1=t1, t2=t2: e.tensor_tensor(out=t1, in0=t1, in1=t2, op=ALU.mult),
                                reads=[b_t[i][0], b_t[i][1]], writes=[b_t[i][0]])
                        for i in range(4):
                            h = 4 * q + i
                            t1 = tmp_view(i, 0)
                            pre = HT[:, h, :]
                            dve(lambda e, t1=t1, pre=pre, h=h: e.tensor_scalar(
                                    out=pre, in0=t1[:, 2:2 + T], scalar1=dww(h, 0), scalar2=dwb(h),
                                    op0=ALU.mult, op1=ALU.add),
                                reads=[b_t[i][0], b_const], writes=[b_ht[h]])
                        for k in DVE_TAPS:
                            for i in range(4):
                                h = 4 * q + i
                                t1 = tmp_view(i, 0)
                                pre = HT[:, h, :]
                                dve(lambda e, t1=t1, pre=pre, h=h, k=k: e.scalar_tensor_tensor(
                                        out=pre, in0=t1[:, 2 + k:2 + k + T], scalar=dww(h, k), in1=pre,
                                        op0=ALU.mult, op1=ALU.add),
                                    reads=[b_t[i][0], b_const], writes=[b_ht[h]])
                        for n_, k in enumerate(POOL_TAPS):
                            for i in range(4):
                                h = 4 * q + i
                                t1 = tmp_view(i, 0)
                                a2 = acc2_view(i)
                                if n_ == 0:
                                    def reg(i=i, h=h, t1=t1, a2=a2, k=k):
                                        sch.op("pool", lambda e: e.tensor_scalar(
                                                   out=a2, in0=t1[:, 2 + k:2 + k + T], scalar1=dww(h, k),
                                                   scalar2=None, op0=ALU.mult),
                                               reads=[b_t[i][0], b_const], writes=[b_acc2[i]])
                                else:
                                    def reg(i=i, h=h, t1=t1, a2=a2, k=k):
                                        sch.op("pool", lambda e: e.scalar_tensor_tensor(
                                                   out=a2, in0=t1[:, 2 + k:2 + k + T], scalar=dww(h, k), in1=a2,
                                                   op0=ALU.mult, op1=ALU.add),
                                               reads=[b_t[i][0], b_const], writes=[b_acc2[i]])
                                pool_pending.append(reg)
                        state["pool_skip"] = 3
                    if kind == "X":
                        for i in range(4):
                            t3 = tmp_view(i, 2)
                            t4 = tmp_view(i, 3)
                            dve(lambda e, t3=t3, t4=t4: e.tensor_tensor(out=t3, in0=t3, in1=t4, op=ALU.mult),
                                reads=[b_t[i][2], b_t[i][3]], writes=[b_t[i][2]])
                        for i in range(4):
                            h = 4 * q + i
                            t3 = tmp_view(i, 2)
                            y = tmp_view(i, 1)[:, 0:T]
                            dve(lambda e, t3=t3, y=y, h=h: e.tensor_scalar(
                                    out=y, in0=t3[:, 30:30 + T], scalar1=scw(h, 0), scalar2=None, op0=ALU.mult),
                                reads=[b_t[i][2], b_const], writes=[b_t[i][1]])
                        for k in (1, 2):
                            for i in range(4):
                                h = 4 * q + i
                                t3 = tmp_view(i, 2)
                                y = tmp_view(i, 1)[:, 0:T]
                                dve(lambda e, t3=t3, y=y, h=h, k=k: e.scalar_tensor_tensor(
                                        out=y, in0=t3[:, 30 + k:30 + k + T], scalar=scw(h, k), in1=y,
                                        op0=ALU.mult, op1=ALU.add),
                                    reads=[b_t[i][2], b_const], writes=[b_t[i][1]])
                    if kind == "B":
                        flush_pool()
                        if POOL_TAPS:
                            for i in range(4):
                                h = 4 * q + i
                                pre = HT[:, h, :]
                                a2 = acc2_view(i)
                                dve(lambda e, pre=pre, a2=a2: e.tensor_tensor(out=pre, in0=pre, in1=a2, op=ALU.add),
                                    reads=[b_ht[h], b_acc2[i]], writes=[b_ht[h]])
                        for i in range(4):
                            h = 4 * q + i
                            pre = HT[:, h, :]
                            act(cb_view(i), pre, AF.Copy, reads=[b_ht[h]], writes=[b_cb[i]])
                            act(sq_view(i), pre, AF.Square, reads=[b_ht[h]], writes=[b_sq4[i]])
                            pending_stats.append(h)
            emit_stats(pending_stats)
            pending_stats = []

            dve(lambda e: e.tensor_scalar(out=TMPF[:, 0, :], in0=PS[:, 6, :], scalar1=1.0 / 2048, scalar2=None,
                                          op0=ALU.mult),
                reads=[b_ps[6]], writes=[b_tmpf[0]])
            dve(lambda e: e.tensor_tensor(out=TMPF[:, 1, :], in0=TMPF[:, 0, :], in1=TMPF[:, 0, :], op=ALU.mult),
                reads=[b_tmpf[0]], writes=[b_tmpf[1]])
            dve(lambda e: e.scalar_tensor_tensor(out=TMPF[:, 1, :], in0=PS[:, 7, :], scalar=1.0 / 2048,
                                                 in1=TMPF[:, 1, :], op0=ALU.mult, op1=ALU.subtract),
                reads=[b_ps[7], b_tmpf[1]], writes=[b_tmpf[1]])
            act(RSTD[:, 1, 0:T], TMPF[:, 1, :], AF.Sqrt, reads=[b_tmpf[1], b_const], writes=[b_rstd[1]],
                bias=eps_ln, scale=1.0)
            dve(lambda e: e.reciprocal(out=RSTD[:, 1, 0:T], in_=RSTD[:, 1, 0:T]),
                reads=[b_rstd[1]], writes=[b_rstd[1]])
            for h0 in range(0, 16, 4):
                for h in range(h0, h0 + 4):
                    pre = HT[:, h, :]
                    dve(lambda e, pre=pre: e.tensor_tensor(out=pre, in0=pre, in1=TMPF[:, 0, :], op=ALU.subtract),
                        reads=[b_ht[h], b_tmpf[0]], writes=[b_ht[h]])
                for h in range(h0, h0 + 4):
                    pre = HT[:, h, :]
                    dve(lambda e, pre=pre: e.tensor_tensor(out=pre, in0=pre, in1=RSTD[:, 1, 0:T], op=ALU.mult),
                        reads=[b_ht[h], b_rstd[1]], writes=[b_ht[h]])
                for h in range(h0, h0 + 4):
                    pre = HT[:, h, :]
                    act(R1[:, h, :], pre, AF.Silu, reads=[b_ht[h], b_const], writes=[b_r1[h]],
                        bias=lnb(h), scale=lng(h))

            for g_ in range(NXP // 4):
                lo_, hi_ = g_ * 4 * T, (g_ + 1) * 4 * T
                utb = [b_ut[c] for c in range(lo_ // (TW // 2), min(KC - 1, (hi_ - 1) // (TW // 2)) + 1)]
                dstp = R2[:, lo_:hi_].rearrange("p (a n) -> p a n", a=4)
                srcp = xT_v[t, :, 4 * g_:4 * g_ + 4, HALO:TW]
                sch.dma("sp", lambda e, dstp=dstp, srcp=srcp: e.dma_start(out=dstp, in_=srcp),
                        xp_sem[g_], writes=[b_xp[c] for c in range(4 * g_, 4 * g_ + 4)] + utb)
            pend = []

            def emit_post_stats(lst):
                for j in lst:
                    i = j % 4
                    mm(PS[:, 7, :], ONES[:, :], sq_view(i), j == 0, j == KC - 1,
                       reads=[b_sq4[i], b_ones], writes=[b_ps[7]], signal=True)

            for jg in range(8):
                for kqi, kq in enumerate((2, 3, 0, 1)):
                    s, wv = load_unit(w_out_v[:, kq, :, jg * 512:(jg + 1) * 512], (8, 512))
                    for i in range(4):
                        for kk in range(8):
                            k = kq * 8 + kk
                            mm(PS[:, i, :], wv[:, kk, i * 128:(i + 1) * 128], R1[:, k, :],
                               (kqi == 0 and kk == 0), (kqi == 3 and kk == 7),
                               reads=[b_ring[s], b_r1[k]], writes=[b_ps[i]], signal=(kk == 7))
                emit_post_stats(pend)
                pend = []
                for i in range(4):
                    j = jg * 4 + i
                    act(HT[:, j, :], PS[:, i, :], AF.Copy, reads=[b_ps[i]], writes=[b_ht[j]])
                    act(sq_view(i), PS[:, i, :], AF.Square, reads=[b_ps[i]], writes=[b_sq4[i]])
                    pend.append(j)
            emit_post_stats(pend)
            pend = []
            stats_rstd([(PS[:, 7, :], slice(0, T), b_ps[7])], D, eps_rms, 0, None)
            for c0 in range(0, KC, 2):
                xis = []
                for c in (c0, c0 + 1):
                    if c < NXP:
                        xis.append((R2[:, c * T:(c + 1) * T], b_xp[c]))
                        continue
                    xi = state["xs"] % 3
                    state["xs"] += 1
                    xis.append((XS[:, xi, 0:T], b_xs[xi]))
                    src = xT_v[t, :, c, HALO:TW]
                    sch.dma("sp", lambda e, xi=xi, src=src: e.dma_start(out=XS[:, xi, 0:T], in_=src),
                            xs_sem[xi], writes=[b_xs[xi]])
                for c in (c0, c0 + 1):
                    hc = HT[:, c, :]
                    dve(lambda e, hc=hc: e.tensor_tensor(out=hc, in0=hc, in1=RSTD[:, 0, 0:T], op=ALU.mult),
                        reads=[b_ht[c], b_rstd[0]], writes=[b_ht[c]])
                for c, (xv_, xb_) in zip((c0, c0 + 1), xis):
                    hc = HT[:, c, :]
                    dve(lambda e, hc=hc, xv_=xv_, c=c: e.scalar_tensor_tensor(
                            out=hc, in0=hc, scalar=cg(1, c), in1=xv_, op0=ALU.mult, op1=ALU.add),
                        reads=[b_ht[c], xb_, b_const], writes=[b_ht[c]])
                for c in (c0, c0 + 1):
                    hc = HT[:, c, :]
                    si = next_sq()
                    act(SQS[:, si, 0:T], hc, AF.Square, reads=[b_ht[c]], writes=[b_sqs[si]])
                    mm(PS[:, 6, :], ONES[:, :], SQS[:, si, 0:T], c == 0, c == KC - 1,
                       reads=[b_sqs[si], b_ones], writes=[b_ps[6]], signal=True)
            stats_rstd([(PS[:, 6, :], slice(0, T), b_ps[6])], D, eps_rms, 1, None)
            for c in range(KC):
                dve(lambda e, c=c: e.scalar_tensor_tensor(
                        out=R1[:, c, :], in0=HT[:, c, :], scalar=cg(2, c), in1=RSTD[:, 1, 0:T],
                        op0=ALU.mult, op1=ALU.mult),
                    reads=[b_ht[c], b_rstd[1], b_const], writes=[b_r1[c]])

            NG = DFF // 512
            fence_c = sch.fence()

            def up_unit(g, kq):
                s, wv = load_unit(w_up_v[:, kq, :, g * 512:(g + 1) * 512], (8, 512))
                for f in range(4):
                    for kk in range(8):
                        k = kq * 8 + kk
                        mm(PS[:, f, :], wv[:, kk, f * 128:(f + 1) * 128], R1[:, k, :],
                           k == 0, k == KC - 1, reads=[b_ring[s], b_r1[k]], writes=[b_ps[f]],
                           signal=(kk == 7))

            def up_evac(g):
                hb = g % 2
                for f in range(4):
                    ti = f % 2
                    act(TMPF[:, ti, :], PS[:, f, :], AF.Relu, reads=[b_ps[f]], writes=[b_tmpf[ti]])
                    act(HID[:, hb, f, :], TMPF[:, ti, :], AF.Square, reads=[b_tmpf[ti]], writes=[b_hid[hb][f]])

            dn_rot = [0]
            final_stats = []

            def flush_final():
                for (j, si) in final_stats:
                    mm(PS[:, 7, :], ONES[:, :], SQS[:, si, 0:T], j == 0, j == KC - 1,
                       reads=[b_sqs[si], b_ones], writes=[b_ps[7]], signal=True)
                final_stats.clear()

            def down_unit(g, jb):
                hb = g % 2
                s, wv = load_unit(w_dn_v[:, g, :, jb * 1024:(jb + 1) * 1024], (4, 1024))
                for jj in range(8):
                    j = jb * 8 + jj
                    bank = 4 + dn_rot[0] % 3
                    dn_rot[0] += 1
                    for f in range(4):
                        mm(PS[:, bank, :], wv[:, f, jj * 128:(jj + 1) * 128], HID[:, hb, f, :],
                           f == 0, f == 3, reads=[b_ring[s], b_hid[hb][f]], writes=[b_ps[bank]],
                           signal=(f == 3))
                    if g == 0:
                        dve(lambda e, j=j, bank=bank: e.tensor_copy(out=ACC[:, j, :], in_=PS[:, bank, :]),
                            reads=[b_ps[bank]], writes=[b_acc[j]], deps=[fence_c])
                    else:
                        dve(lambda e, j=j, bank=bank: e.tensor_tensor(
                                out=ACC[:, j, :], in0=ACC[:, j, :], in1=PS[:, bank, :], op=ALU.add),
                            reads=[b_ps[bank], b_acc[j]], writes=[b_acc[j]])
                    if g == NG - 1:
                        si = next_sq()
                        act(SQS[:, si, 0:T], ACC[:, j, :], AF.Square, reads=[b_acc[j]], writes=[b_sqs[si]])
                        final_stats.append((j, si))
                        if len(final_stats) == 2:
                            flush_final()

            for kq in range(4):
                up_unit(0, kq)
            up_evac(0)
            for g in range(1, NG):
                for u in range(4):
                    up_unit(g, u)
                    down_unit(g - 1, u)
                up_evac(g)
            for jb in range(4):
                down_unit(NG - 1, jb)
            flush_final()

            stats_rstd([(PS[:, 7, :], slice(0, T), b_ps[7])], D, eps_rms, 0, None)
            for j0 in range(0, KC, 2):
                for j in (j0, j0 + 1):
                    aj = ACC[:, j, :]
                    dve(lambda e, aj=aj: e.tensor_tensor(out=aj, in0=aj, in1=RSTD[:, 0, 0:T], op=ALU.mult),
                        reads=[b_acc[j], b_rstd[0]], writes=[b_acc[j]])
                for j in (j0, j0 + 1):
                    aj = ACC[:, j, :]
                    dve(lambda e, aj=aj, j=j: e.scalar_tensor_tensor(
                            out=aj, in0=aj, scalar=cg(3, j), in1=HT[:, j, :], op0=ALU.mult, op1=ALU.add),
                        reads=[b_acc[j], b_ht[j], b_const], writes=[b_acc[j]])
                aj2 = ACC[:, j0:j0 + 2, :]
                oi = state["out"] % 4
                state["out"] += 1
                dst = outT_v[:, j0:j0 + 2, t * T:(t + 1) * T]
                tk = sch.dma("sp", lambda e, aj2=aj2, dst=dst: e.dma_start(out=dst, in_=aj2),
                             out_sem[oi], reads=[b_acc[j0], b_acc[j0 + 1]])
                store_tok[j0] = tk
                store_tok[j0 + 1] = tk
            tile_fence = sch.fence()

        total_wait = sch.fence()
        with nc.Block() as block:
            def make_body(eng):
                def body(e):
                    known = {}
                    for fn, toks, sig in sch.ops[eng]:
                        for k, v in toks.items():
                            if known.get(k, 0) >= v:
                                continue
                            e.wait_ge(sems[k], v)
                            known[k] = v
                        inst = fn(e)
                        if sig is not None:
                            inst.then_inc(sems[sig[0]], sig[1])
                    if eng == "sp":
                        for k in out_sem:
                            if sch.dma_cnt.get(k, 0):
                                e.wait_ge(sems[k], sch.dma_cnt[k])
                return body
            block.tensor(make_body("pe"))
            block.scalar(make_body("act"))
            block.vector(make_body("dve"))
            block.gpsimd(make_body("pool"))
            block.sync(make_body("sp"))
    return nc


def _prep_inputs(x, norm_mix_pre, w_in, conf_dw_w, conf_dw_b, conf_ln_g, conf_ln_b,
                 sc_conv_w, w_out, norm_mix_post, norm_mlp_pre, w_up, w_down, norm_mlp_post):
    f = np.float32
    x2 = np.asarray(x, dtype=f).reshape(S, D)
    xpad = np.concatenate([np.zeros((HALO, D), f), x2], axis=0)
    cst = np.zeros((128, NCF), f)
    for n, g in enumerate([norm_mix_pre, norm_mix_post, norm_mlp_pre, norm_mlp_post]):
        cst[:, C_G + n * 32:C_G + (n + 1) * 32] = np.asarray(g, f).reshape(32, 128).T
    cst[:, C_DWW:C_DWW + 496] = np.asarray(conf_dw_w, f).reshape(31, 16, 128).transpose(2, 1, 0).reshape(128, 496)
    cst[:, C_DWB:C_DWB + 16] = np.asarray(conf_dw_b, f).reshape(16, 128).T
    cst[:, C_LNG:C_LNG + 16] = np.asarray(conf_ln_g, f).reshape(16, 128).T
    cst[:, C_LNB:C_LNB + 16] = np.asarray(conf_ln_b, f).reshape(16, 128).T
    cst[:, C_SCW:C_SCW + 48] = np.asarray(sc_conv_w, f).reshape(3, 16, 128).transpose(2, 1, 0).reshape(128, 48)
    cst[:, C_EPS] = RMS_EPS
    cst[:, C_EPS + 1] = LN_EPS
    w_in = np.ascontiguousarray(w_in, dtype=f)
    w_out = np.ascontiguousarray(w_out, dtype=f)
    w_up = np.ascontiguousarray(w_up, dtype=f)
    w_down = np.ascontiguousarray(w_down, dtype=f)
    in_maps = []
    for c in range(NCORES):
        xt = np.empty((NT, D, TW), f)
        for t in range(NT):
            s0 = c * TPC + t * T
            xt[t] = xpad[s0:s0 + TW, :].T
        in_maps.append({"xT": xt, "w_in": w_in, "w_out": w_out, "w_up": w_up, "w_down": w_down, "cst": cst})
    return in_maps


_NC_CACHE = {}


def kernel(**inputs):
    in_maps = _prep_inputs(**inputs)
    if "nc" not in _NC_CACHE:
        _NC_CACHE["nc"] = build_program()
    nc = _NC_CACHE["nc"]
    res = run_bass_kernel_spmd(nc, in_maps, core_ids=list(range(NCORES)))
    outs = [np.asarray(r["outT"], dtype=np.float32).T for r in res.results]
    return np.concatenate(outs, axis=0).reshape(1, S, D)
```

```python
import numpy as np
from contextlib import ExitStack

import concourse.bass as bass
import concourse.mybir as mybir
from concourse.bass_utils import run_bass_kernel_spmd

F32 = mybir.dt.float32
BF16 = mybir.dt.bfloat16
ALU = mybir.AluOpType
AF = mybir.ActivationFunctionType

NCORES = 8
D = 4096
S = 8192
TPC = S // NCORES
T = 512
NT = TPC // T
HALO = 32
TW = T + HALO
KC = D // 128
EIN = 10240
DFF = 16384
NR = 3
RMS_EPS = 1e-6
LN_EPS = 1e-5

C_G = 0
C_DWW = 128
C_DWB = 624
C_LNG = 640
C_LNB = 656
C_SCW = 672
C_EPS = 720
NCF = 768

ENGS = ["pe", "act", "dve", "pool", "sp"]


class Buf:
    __slots__ = ("w", "r", "name")

    def __init__(self, name=""):
        self.w = None
        self.r = {}
        self.name = name


class Sched:
    def __init__(self):
        self.ops = {e: [] for e in ENGS}
        self.cnt = {e: 0 for e in ENGS}
        self.dma_cnt = {}

    def _deps(self, reads, writes, deps):
        toks = {}

        def add(tok):
            if tok is None:
                return
            k, v = tok
            if toks.get(k, 0) < v:
                toks[k] = v
        for d in deps:
            if isinstance(d, dict):
                for k, v in d.items():
                    add((k, v))
            else:
                add(d)
        for b in reads:
            add(b.w)
        for b in writes:
            add(b.w)
            for k, v in b.r.items():
                add((k, v))
        return toks

    def _commit(self, tok, reads, writes):
        k, v = tok
        for b in reads:
            if b.r.get(k, 0) < v:
                b.r[k] = v
        for b in writes:
            b.w = tok
            b.r = {}

    def op(self, eng, fn, reads=(), writes=(), signal=True, deps=()):
        toks = self._deps(reads, writes, deps)
        if eng == "pe":
            toks.pop("pe", None)
        tok = (eng, self.cnt[eng] + 1)
        if signal:
            self.cnt[eng] += 1
        self.ops[eng].append((fn, toks, (eng, 1) if signal else None))
        self._commit(tok, reads, writes)
        return tok

    def dma(self, queue, fn, semkey, reads=(), writes=(), deps=()):
        prev = self.dma_cnt.get(semkey, 0)
        toks = self._deps(reads, writes, deps)
        if prev:
            if toks.get(semkey, 0) < prev:
                toks[semkey] = prev
        self.dma_cnt[semkey] = prev + 16
        tok = (semkey, prev + 16)
        self.ops[queue].append((fn, toks, (semkey, 16)))
        self._commit(tok, reads, writes)
        return tok

    def fence(self):
        f = dict(self.cnt)
        f.update(self.dma_cnt)
        return {k: v for k, v in f.items() if v > 0}


def build_program():
    nc = bass.Bass("TRN2", target_bir_lowering=False)
    xT = nc.dram_tensor("xT", [NT, D, TW], F32, kind="ExternalInput").ap()
    w_in = nc.dram_tensor("w_in", [D, EIN], F32, kind="ExternalInput").ap()
    w_out = nc.dram_tensor("w_out", [D, D], F32, kind="ExternalInput").ap()
    w_up = nc.dram_tensor("w_up", [D, DFF], F32, kind="ExternalInput").ap()
    w_down = nc.dram_tensor("w_down", [DFF, D], F32, kind="ExternalInput").ap()
    cst = nc.dram_tensor("cst", [128, NCF], F32, kind="ExternalInput").ap()
    outT = nc.dram_tensor("outT", [D, TPC], F32, kind="ExternalOutput").ap()

    w_in_v = w_in.rearrange("(kq kk p) e -> p kq kk e", p=128, kk=8)
    w_out_v = w_out.rearrange("(kq kk p) e -> p kq kk e", p=128, kk=8)
    w_up_v = w_up.rearrange("(kq kk p) e -> p kq kk e", p=128, kk=8)
    w_dn_v = w_down.rearrange("(g f p) e -> p g f e", p=128, f=4)
    xT_v = xT.rearrange("t (c p) n -> t p c n", p=128)
    outT_v = outT.rearrange("(c p) n -> p c n", p=128)

    es = ExitStack()
    with es:
        CONSTF = es.enter_context(nc.sbuf_tensor("constf", [128, NCF], F32))
        ONES = es.enter_context(nc.sbuf_tensor("ones", [128, 128], BF16))
        RING = es.enter_context(nc.sbuf_tensor("ring", [128, NR, 4096], BF16))
        HT = es.enter_context(nc.sbuf_tensor("ht", [128, KC, T], F32))
        R1 = es.enter_context(nc.sbuf_tensor("r1", [128, KC, T], BF16))
        R2 = es.enter_context(nc.sbuf_tensor("r2", [128, KC * T], F32))
        HID = es.enter_context(nc.sbuf_tensor("hid", [128, 2, 4, T], BF16))
        TMPF = es.enter_context(nc.sbuf_tensor("tmpf", [128, 2, T], F32))
        RSTD = es.enter_context(nc.sbuf_tensor("rstd", [128, 2, TW], F32))
        SQS = es.enter_context(nc.sbuf_tensor("sqs", [128, 4, TW], BF16))
        PS = es.enter_context(nc.psum_tensor("ps", [128, 8, 512], F32))

        sems = {}
        for e in ENGS:
            sems[e] = es.enter_context(nc.semaphore("s_" + e))

        def dsem(key):
            if key not in sems:
                sems[key] = es.enter_context(nc.semaphore("d_" + key))
            return key

        sch = Sched()

        R2b = R2.bitcast(BF16)
        UT = R2b[:, 0:KC * TW].rearrange("p (c n) -> p c n", c=KC)
        XS0 = KC * TW // 2
        XS = R2[:, XS0:XS0 + 3 * TW].rearrange("p (a n) -> p a n", a=3)
        TB0 = XS0 + 3 * TW
        ACC = R2[:, :].rearrange("p (c n) -> p c n", c=KC)
        HTflat = HT[:, :, :].rearrange("p c n -> p (c n)")
        TBASE = 16 * T

        def tmp_view(i, which):
            if which < 3:
                off = TBASE + i * (3 * TW) + which * TW
                return HTflat[:, off:off + TW]
            off = TB0 + i * TW
            return R2[:, off:off + TW]

        def cb_view(i):
            off = TB0 + 4 * TW + i * 256
            return R2[:, off:off + 256].bitcast(BF16)

        def sq_view(i):
            off = TB0 + 4 * TW + 4 * 256 + i * 256
            return R2[:, off:off + 256].bitcast(BF16)

        cg = lambda n, c: CONSTF[:, C_G + n * 32 + c:C_G + n * 32 + c + 1]
        dww = lambda j, k: CONSTF[:, C_DWW + j * 31 + k:C_DWW + j * 31 + k + 1]
        dwb = lambda j: CONSTF[:, C_DWB + j:C_DWB + j + 1]
        lng = lambda j: CONSTF[:, C_LNG + j:C_LNG + j + 1]
        lnb = lambda j: CONSTF[:, C_LNB + j:C_LNB + j + 1]
        scw = lambda j, k: CONSTF[:, C_SCW + j * 3 + k:C_SCW + j * 3 + k + 1]
        eps_rms = CONSTF[:, C_EPS:C_EPS + 1]
        eps_ln = CONSTF[:, C_EPS + 1:C_EPS + 2]

        b_const = Buf("const")
        b_ones = Buf("ones")
        b_ring = [Buf("ring%d" % i) for i in range(NR)]
        b_ps = [Buf("ps%d" % i) for i in range(8)]
        b_halo = [b_ps[4]] * 4
        b_ht = [Buf("ht%d" % i) for i in range(KC)]
        b_r1 = [Buf("r1_%d" % i) for i in range(KC)]
        b_ut = [Buf("ut%d" % i) for i in range(KC)]
        b_acc = [Buf("acc%d" % i) for i in range(KC)]
        b_xs = [Buf("xs%d" % i) for i in range(3)]
        b_hid = [[Buf("hid%d_%d" % (i, f)) for f in range(4)] for i in range(2)]
        b_tmpf = [Buf("tmpf%d" % i) for i in range(2)]
        b_rstd = [Buf("rstd%d" % i) for i in range(2)]
        b_sqs = [Buf("sqs%d" % i) for i in range(4)]
        b_t = [[Buf("t%d_%d" % (i, w)) for w in range(4)] for i in range(4)]
        b_cb = [Buf("cb%d" % i) for i in range(4)]
        b_sq4 = [Buf("sq4_%d" % i) for i in range(4)]

        ring_sem = [dsem("ring%d" % i) for i in range(NR)]
        xs_sem = [dsem("xs%d" % i) for i in range(3)]
        out_sem = [dsem("out%d" % i) for i in range(4)]
        cst_sem = dsem("cst")

        state = {"unit": 0, "sq": 0, "xs": 0, "out": 0, "pool_skip": 0}
        pool_pending = []
        POOL_PER_UNIT = 5
        POOL_TAPS = []
        DVE_TAPS = list(range(1, 31))
        b_acc2 = [Buf("acc2_%d" % i) for i in range(4)]

        def acc2_view(i):
            if i < 3:
                off = TB0 + 4 * TW + 8 * 256 + i * T
                return R2[:, off:off + T]
            off = TBASE + 4 * 3 * TW
            return HTflat[:, off:off + T]


        def load_unit(src_ap, shape3):
            u = state["unit"]
            state["unit"] += 1
            s = u % NR
            a, c = shape3
            dst = RING[:, s, :].rearrange("p (a c) -> p a c", a=a)
            sch.dma("pool", lambda e, dst=dst, src=src_ap: e.dma_start(out=dst, in_=src),
                    ring_sem[s], writes=[b_ring[s]])
            if state["pool_skip"] > 0:
                state["pool_skip"] -= 1
            else:
                n = 0
                while pool_pending and n < POOL_PER_UNIT:
                    pool_pending.pop(0)()
                    n += 1
            return s, dst

        def flush_pool():
            while pool_pending:
                pool_pending.pop(0)()

        def mm(out, lhsT, rhs, start, stop, reads, writes, signal, skip=False):
            if skip:
                fn = lambda e: e.matmul(out, lhsT, rhs, start=start, stop=stop, skip_group_check=True)
            else:
                fn = lambda e: e.matmul(out, lhsT, rhs, start=start, stop=stop)
            return sch.op("pe", fn, reads=reads, writes=writes, signal=signal)

        def act(out, in_, func, reads, writes, bias=None, scale=None, deps=()):
            kw = {}
            if bias is not None:
                kw["bias"] = bias
            if scale is not None:
                kw["scale"] = scale
            return sch.op("act", lambda e: e.activation(out, in_, func, **kw), reads=reads, writes=writes, deps=deps)

        def dve(fn, reads, writes, deps=()):
            return sch.op("dve", fn, reads=reads, writes=writes, deps=deps)

        def next_sq():
            i = state["sq"] % 4
            state["sq"] += 1
            return i

        def stats_rstd(bank_ap_list, n_feat, eps_ap, rstd_idx, bufs_read):
            for ps_ap, sl, pb in bank_ap_list:
                o = RSTD[:, rstd_idx, sl]
                act(o, ps_ap, AF.Sqrt, reads=[pb, b_const], writes=[b_rstd[rstd_idx]],
                    bias=eps_ap, scale=1.0 / n_feat)
            o = RSTD[:, rstd_idx, :] if len(bank_ap_list) == 2 else RSTD[:, rstd_idx, bank_ap_list[0][1]]
            dve(lambda e, o=o: e.reciprocal(out=o, in_=o), reads=[b_rstd[rstd_idx]], writes=[b_rstd[rstd_idx]])

        sch.dma("sp", lambda e: e.dma_start(out=CONSTF[:, :], in_=cst), cst_sem, writes=[b_const])
        dve(lambda e: e.memset(ONES[:, :], 1.0), reads=[], writes=[b_ones])

        TMPFflat = TMPF[:, :, :].rearrange("p a n -> p (a n)")
        XGROUPS = [(30, 1), (31, 1)] + [(c0, 3) for c0 in range(0, 30, 3)]
        xg_sem = [dsem("xg%d" % i) for i in range(len(XGROUPS))]

        b_xst = [Buf("xst%d" % i) for i in range(KC)]

        def x_region(c0, n):
            if c0 == 30:
                return RSTD[:, 1, :], [b_rstd[1]]
            if c0 == 31:
                return TMPFflat[:, 0:TW], [b_tmpf[0], b_tmpf[1]]
            lo, hi = c0 * TW, (c0 + n) * TW
            return HTflat[:, lo:hi], [b_xst[c] for c in range(c0, c0 + n)]

        def x_ht_bufs(c):
            if c >= 30:
                return []
            lo, hi = c * TW, (c + 1) * TW
            return [b_ht[i] for i in range(lo // T, (hi - 1) // T + 1)]

        def make_pass1(t, tf):
            st = {"gi": 0}

            def need_ht(c0, n):
                return -1 if c0 >= 30 else ((c0 + n) * TW - 1) // T

            def advance(limit):
                while st["gi"] < len(XGROUPS):
                    gi = st["gi"]
                    c0, n = XGROUPS[gi]
                    if need_ht(c0, n) > limit:
                        break
                    reg, bufs = x_region(c0, n)
                    dst = reg.rearrange("p (a n) -> p a n", a=n)
                    src = xT_v[t, :, c0:c0 + n, :]
                    sch.dma("sp", lambda e, dst=dst, src=src: e.dma_start(out=dst, in_=src),
                            xg_sem[gi], writes=bufs, deps=[tf])
                    for a_ in range(n):
                        c = c0 + a_
                        xv, xb = x_region(c, 1)
                        si = next_sq()
                        if c % 2 == 0:
                            act(SQS[:, si, :], xv, AF.Square, reads=xb + x_ht_bufs(c), writes=[b_sqs[si]], deps=[tf])
                        else:
                            dve(lambda e, xv=xv, si=si: e.tensor_tensor(out=SQS[:, si, :], in0=xv, in1=xv, op=ALU.mult),
                                reads=xb + x_ht_bufs(c), writes=[b_sqs[si]], deps=[tf])
                        first = (gi == 0 and a_ == 0)
                        last = (gi == len(XGROUPS) - 1 and a_ == n - 1)
                        mm(PS[:, 6, :], ONES[:, :], SQS[:, si, HALO:TW], first, last,
                           reads=[b_sqs[si], b_ones], writes=[b_ps[6]], signal=False)
                        mm(PS[:, 4, 0:HALO], ONES[:, :], SQS[:, si, 0:HALO], first, last,
                           reads=[b_sqs[si], b_ones], writes=[b_halo[0]], signal=True)
                    st["gi"] += 1
            return advance

        store_tok = {}
        tile_fence = {}
        for t in range(NT):
            tf = dict(tile_fence)
            make_pass1(t, tf)(KC)
            stats_rstd([(PS[:, 6, :], slice(HALO, TW), b_ps[6]), (PS[:, 4, 0:HALO], slice(0, HALO), b_halo[0])],
                       D, eps_rms, 0, None)
            for c in range(KC):
                xv, xb = x_region(c, 1)
                lo, hi = c * (TW // 2), (c + 1) * (TW // 2)
                deps = [store_tok[a_] for a_ in range(lo // T, (hi - 1) // T + 1) if a_ in store_tok]
                if c == KC - 1:
                    deps = list(store_tok.values())
                dve(lambda e, xv=xv, c=c: e.scalar_tensor_tensor(
                        out=UT[:, c, :], in0=xv, scalar=cg(0, c), in1=RSTD[:, 0, :],
                        op0=ALU.mult, op1=ALU.mult),
                    reads=xb + x_ht_bufs(c) + [b_rstd[0], b_const], writes=[b_ut[c]], deps=deps + [tf])
            store_tok = {}

            pending_stats = []

            def emit_stats(lst):
                for h in lst:
                    i = h % 4
                    mm(PS[:, 6, :], ONES[:, :], cb_view(i), h == 0, h == 15,
                       reads=[b_cb[i], b_ones], writes=[b_ps[6]], signal=False)
                    mm(PS[:, 7, :], ONES[:, :], sq_view(i), h == 0, h == 15,
                       reads=[b_sq4[i], b_ones], writes=[b_ps[7]], signal=True)

            halo_rot = 0
            for q in range(4):
                for kind in ("V", "G", "C", "X", "B"):
                    e0 = {"V": 0, "G": 16, "B": 32, "C": 48, "X": 64}[kind] + 4 * q
                    has_halo = kind != "B"
                    hr = 0
                    if has_halo:
                        hr = halo_rot % 4
                        halo_rot += 1
                    for kq in range(4):
                        s, wv = load_unit(w_in_v[:, kq, :, e0 * 128:e0 * 128 + 512], (8, 512))
                        for i in range(4):
                            for kk in range(8):
                                k = kq * 8 + kk
                                last = (kk == 7) and (not has_halo)
                                mm(PS[:, i, :], wv[:, kk, i * 128:(i + 1) * 128], UT[:, k, HALO:TW],
                                   k == 0, k == KC - 1, reads=[b_ring[s], b_ut[k]], writes=[b_ps[i]],
                                   signal=last)
                            if has_halo:
                                for kk in range(8):
                                    k = kq * 8 + kk
                                    mm(PS[:, 4, hr * 128 + i * 32:hr * 128 + (i + 1) * 32],
                                       wv[:, kk, i * 128:(i + 1) * 128], UT[:, k, 0:HALO],
                                       (k == 0 and i == 0), k == KC - 1,
                                       reads=[b_ring[s], b_ut[k]], writes=[b_halo[hr]],
                                       signal=(kk == 7), skip=True)
                    if kind == "G" and pending_stats:
                        emit_stats(pending_stats)
                        pending_stats = []
                    for i in range(4):
                        h = 4 * q + i
                        main = PS[:, i, :]
                        halo = PS[:, 4, hr * 128 + i * 32:hr * 128 + (i + 1) * 32]
                        if kind in ("V", "G", "C", "X"):
                            w = {"V": 0, "G": 1, "C": 2, "X": 3}[kind]
                            fnc = AF.Sigmoid if kind == "G" else AF.Copy
                            tv = tmp_view(i, w)
                            act(tv[:, HALO:TW], main, fnc, reads=[b_ps[i]], writes=[b_t[i][w]])
                            act(tv[:, 0:HALO], halo, fnc, reads=[b_halo[hr]], writes=[b_t[i][w]])
                        else:
                            y = tmp_view(i, 1)[:, 0:T]
                            dve(lambda e, y=y, main=main, h=h: e.tensor_tensor(
                                    out=R1[:, 16 + h, :], in0=main, in1=y, op=ALU.mult),
                                reads=[b_ps[i], b_t[i][1]], writes=[b_r1[16 + h]])
                    if kind == "G":
                        for i in range(4):
                            t1 = tmp_view(i, 0)
                            t2 = tmp_view(i, 1)
                            dve(lambda e, t1=t1, t2=t2: e.tensor_tensor(out=t1, in0=t1, in1=t2, op=ALU.mult),
                                reads=[b_t[i][0], b_t[i][1]], writes=[b_t[i][0]])
                        for i in range(4):
                            h = 4 * q + i
                            t1 = tmp_view(i, 0)
                            pre = HT[:, h, :]
                            dve(lambda e, t1=t1, pre=pre, h=h: e.tensor_scalar(
                                    out=pre, in0=t1[:, 2:2 + T], scalar1=dww(h, 0), scalar2=dwb(h),
                                    op0=ALU.mult, op1=ALU.add),
                                reads=[b_t[i][0], b_const], writes=[b_ht[h]])
                        for k in DVE_TAPS:
                            for i in range(4):
                                h = 4 * q + i
                                t1 = tmp_view(i, 0)
                                pre = HT[:, h, :]
                                dve(lambda e, t1=t1, pre=pre, h=h, k=k: e.scalar_tensor_tensor(
                                        out=pre, in0=t1[:, 2 + k:2 + k + T], scalar=dww(h, k), in1=pre,
                                        op0=ALU.mult, op1=ALU.add),
                                    reads=[b_t[i][0], b_const], writes=[b_ht[h]])
                        for n_, k in enumerate(POOL_TAPS):
                            for i in range(4):
                                h = 4 * q + i
                                t1 = tmp_view(i, 0)
                                a2 = acc2_view(i)
                                if n_ == 0:
                                    def reg(i=i, h=h, t1=t1, a2=a2, k=k):
                                        sch.op("pool", lambda e: e.tensor_scalar(
                                                   out=a2, in0=t1[:, 2 + k:2 + k + T], scalar1=dww(h, k),
                                                   scalar2=None, op0=ALU.mult),
                                               reads=[b_t[i][0], b_const], writes=[b_acc2[i]])
                                else:
                                    def reg(i=i, h=h, t1=t1, a2=a2, k=k):
                                        sch.op("pool", lambda e: e.scalar_tensor_tensor(
                                                   out=a2, in0=t1[:, 2 + k:2 + k + T], scalar=dww(h, k), in1=a2,
                                                   op0=ALU.mult, op1=ALU.add),
                                               reads=[b_t[i][0], b_const], writes=[b_acc2[i]])
                                pool_pending.append(reg)
                        state["pool_skip"] = 3
                    if kind == "X":
                        for i in range(4):
                            t3 = tmp_view(i, 2)
                            t4 = tmp_view(i, 3)
                            dve(lambda e, t3=t3, t4=t4: e.tensor_tensor(out=t3, in0=t3, in1=t4, op=ALU.mult),
                                reads=[b_t[i][2], b_t[i][3]], writes=[b_t[i][2]])
                        for i in range(4):
                            h = 4 * q + i
                            t3 = tmp_view(i, 2)
                            y = tmp_view(i, 1)[:, 0:T]
                            dve(lambda e, t3=t3, y=y, h=h: e.tensor_scalar(
                                    out=y, in0=t3[:, 30:30 + T], scalar1=scw(h, 0), scalar2=None, op0=ALU.mult),
                                reads=[b_t[i][2], b_const], writes=[b_t[i][1]])
                        for k in (1, 2):
                            for i in range(4):
                                h = 4 * q + i
                                t3 = tmp_view(i, 2)
                                y = tmp_view(i, 1)[:, 0:T]
                                dve(lambda e, t3=t3, y=y, h=h, k=k: e.scalar_tensor_tensor(
                                        out=y, in0=t3[:, 30 + k:30 + k + T], scalar=scw(h, k), in1=y,
                                        op0=ALU.mult, op1=ALU.add),
                                    reads=[b_t[i][2], b_const], writes=[b_t[i][1]])
                    if kind == "B":
                        flush_pool()
                        if POOL_TAPS:
                            for i in range(4):
                                h = 4 * q + i
                                pre = HT[:, h, :]
                                a2 = acc2_view(i)
                                dve(lambda e, pre=pre, a2=a2: e.tensor_tensor(out=pre, in0=pre, in1=a2, op=ALU.add),
                                    reads=[b_ht[h], b_acc2[i]], writes=[b_ht[h]])
                        for i in range(4):
                            h = 4 * q + i
                            pre = HT[:, h, :]
                            act(cb_view(i), pre, AF.Copy, reads=[b_ht[h]], writes=[b_cb[i]])
                            act(sq_view(i), pre, AF.Square, reads=[b_ht[h]], writes=[b_sq4[i]])
                            pending_stats.append(h)
            emit_stats(pending_stats)
            pending_stats = []

            dve(lambda e: e.tensor_scalar(out=TMPF[:, 0, :], in0=PS[:, 6, :], scalar1=1.0 / 2048, scalar2=None,
                                          op0=ALU.mult),
                reads=[b_ps[6]], writes=[b_tmpf[0]])
            dve(lambda e: e.tensor_tensor(out=TMPF[:, 1, :], in0=TMPF[:, 0, :], in1=TMPF[:, 0, :], op=ALU.mult),
                reads=[b_tmpf[0]], writes=[b_tmpf[1]])
            dve(lambda e: e.scalar_tensor_tensor(out=TMPF[:, 1, :], in0=PS[:, 7, :], scalar=1.0 / 2048,
                                                 in1=TMPF[:, 1, :], op0=ALU.mult, op1=ALU.subtract),
                reads=[b_ps[7], b_tmpf[1]], writes=[b_tmpf[1]])
            act(RSTD[:, 1, 0:T], TMPF[:, 1, :], AF.Sqrt, reads=[b_tmpf[1], b_const], writes=[b_rstd[1]],
                bias=eps_ln, scale=1.0)
            dve(lambda e: e.reciprocal(out=RSTD[:, 1, 0:T], in_=RSTD[:, 1, 0:T]),
                reads=[b_rstd[1]], writes=[b_rstd[1]])
            for h0 in range(0, 16, 4):
                for h in range(h0, h0 + 4):
                    pre = HT[:, h, :]
                    dve(lambda e, pre=pre: e.tensor_tensor(out=pre, in0=pre, in1=TMPF[:, 0, :], op=ALU.subtract),
                        reads=[b_ht[h], b_tmpf[0]], writes=[b_ht[h]])
                for h in range(h0, h0 + 4):
                    pre = HT[:, h, :]
                    dve(lambda e, pre=pre: e.tensor_tensor(out=pre, in0=pre, in1=RSTD[:, 1, 0:T], op=ALU.mult),
                        reads=[b_ht[h], b_rstd[1]], writes=[b_ht[h]])
                for h in range(h0, h0 + 4):
                    pre = HT[:, h, :]
                    act(R1[:, h, :], pre, AF.Silu, reads=[b_ht[h], b_const], writes=[b_r1[h]],
                        bias=lnb(h), scale=lng(h))

            pend = []

            def emit_post_stats(lst):
                for j in lst:
                    i = j % 4
                    mm(PS[:, 7, :], ONES[:, :], sq_view(i), j == 0, j == KC - 1,
                       reads=[b_sq4[i], b_ones], writes=[b_ps[7]], signal=True)

            for jg in range(8):
                for kqi, kq in enumerate((2, 3, 0, 1)):
                    s, wv = load_unit(w_out_v[:, kq, :, jg * 512:(jg + 1) * 512], (8, 512))
                    for i in range(4):
                        for kk in range(8):
                            k = kq * 8 + kk
                            mm(PS[:, i, :], wv[:, kk, i * 128:(i + 1) * 128], R1[:, k, :],
                               (kqi == 0 and kk == 0), (kqi == 3 and kk == 7),
                               reads=[b_ring[s], b_r1[k]], writes=[b_ps[i]], signal=(kk == 7))
                emit_post_stats(pend)
                pend = []
                for i in range(4):
                    j = jg * 4 + i
                    act(HT[:, j, :], PS[:, i, :], AF.Copy, reads=[b_ps[i]], writes=[b_ht[j]])
                    act(sq_view(i), PS[:, i, :], AF.Square, reads=[b_ps[i]], writes=[b_sq4[i]])
                    pend.append(j)
            emit_post_stats(pend)
            pend = []
            stats_rstd([(PS[:, 7, :], slice(0, T), b_ps[7])], D, eps_rms, 0, None)
            for c0 in range(0, KC, 2):
                xis = []
                for c in (c0, c0 + 1):
                    xi = state["xs"] % 3
                    state["xs"] += 1
                    xis.append(xi)
                    src = xT_v[t, :, c, HALO:TW]
                    sch.dma("sp", lambda e, xi=xi, src=src: e.dma_start(out=XS[:, xi, 0:T], in_=src),
                            xs_sem[xi], writes=[b_xs[xi]])
                for c in (c0, c0 + 1):
                    hc = HT[:, c, :]
                    dve(lambda e, hc=hc: e.tensor_tensor(out=hc, in0=hc, in1=RSTD[:, 0, 0:T], op=ALU.mult),
                        reads=[b_ht[c], b_rstd[0]], writes=[b_ht[c]])
                for c, xi in zip((c0, c0 + 1), xis):
                    hc = HT[:, c, :]
                    dve(lambda e, hc=hc, xi=xi, c=c: e.scalar_tensor_tensor(
                            out=hc, in0=hc, scalar=cg(1, c), in1=XS[:, xi, 0:T], op0=ALU.mult, op1=ALU.add),
                        reads=[b_ht[c], b_xs[xi], b_const], writes=[b_ht[c]])
                for c in (c0, c0 + 1):
                    hc = HT[:, c, :]
                    si = next_sq()
                    act(SQS[:, si, 0:T], hc, AF.Square, reads=[b_ht[c]], writes=[b_sqs[si]])
                    mm(PS[:, 6, :], ONES[:, :], SQS[:, si, 0:T], c == 0, c == KC - 1,
                       reads=[b_sqs[si], b_ones], writes=[b_ps[6]], signal=True)
            stats_rstd([(PS[:, 6, :], slice(0, T), b_ps[6])], D, eps_rms, 1, None)
            for c in range(KC):
                dve(lambda e, c=c: e.scalar_tensor_tensor(
                        out=R1[:, c, :], in0=HT[:, c, :], scalar=cg(2, c), in1=RSTD[:, 1, 0:T],
                        op0=ALU.mult, op1=ALU.mult),
                    reads=[b_ht[c], b_rstd[1], b_const], writes=[b_r1[c]])

            NG = DFF // 512
            fence_c = sch.fence()

            def up_unit(g, kq):
                s, wv = load_unit(w_up_v[:, kq, :, g * 512:(g + 1) * 512], (8, 512))
                for f in range(4):
                    for kk in range(8):
                        k = kq * 8 + kk
                        mm(PS[:, f, :], wv[:, kk, f * 128:(f + 1) * 128], R1[:, k, :],
                           k == 0, k == KC - 1, reads=[b_ring[s], b_r1[k]], writes=[b_ps[f]],
                           signal=(kk == 7))

            def up_evac(g):
                hb = g % 2
                for f in range(4):
                    ti = f % 2
                    act(TMPF[:, ti, :], PS[:, f, :], AF.Relu, reads=[b_ps[f]], writes=[b_tmpf[ti]])
                    act(HID[:, hb, f, :], TMPF[:, ti, :], AF.Square, reads=[b_tmpf[ti]], writes=[b_hid[hb][f]])

            dn_rot = [0]
            final_stats = []

            def flush_final(n=None):
                lst = final_stats[:n] if n else list(final_stats)
                for (j, si) in lst:
                    mm(PS[:, 7, :], ONES[:, :], SQS[:, si, 0:T], j == 0, j == KC - 1,
                       reads=[b_sqs[si], b_ones], writes=[b_ps[7]], signal=True)
                del final_stats[:len(lst)]

            def down_unit(g, jb):
                hb = g % 2
                s, wv = load_unit(w_dn_v[:, g, :, jb * 1024:(jb + 1) * 1024], (4, 1024))
                for jj in range(8):
                    j = jb * 8 + jj
                    bank = 4 + dn_rot[0] % 3
                    dn_rot[0] += 1
                    for f in range(4):
                        mm(PS[:, bank, :], wv[:, f, jj * 128:(jj + 1) * 128], HID[:, hb, f, :],
                           f == 0, f == 3, reads=[b_ring[s], b_hid[hb][f]], writes=[b_ps[bank]],
                           signal=(f == 3))
                    if g == 0:
                        dve(lambda e, j=j, bank=bank: e.tensor_copy(out=ACC[:, j, :], in_=PS[:, bank, :]),
                            reads=[b_ps[bank]], writes=[b_acc[j]], deps=[fence_c])
                    else:
                        dve(lambda e, j=j, bank=bank: e.tensor_tensor(
                                out=ACC[:, j, :], in0=ACC[:, j, :], in1=PS[:, bank, :], op=ALU.add),
                            reads=[b_ps[bank], b_acc[j]], writes=[b_acc[j]])
                    if g == NG - 1:
                        si = next_sq()
                        act(SQS[:, si, 0:T], ACC[:, j, :], AF.Square, reads=[b_acc[j]], writes=[b_sqs[si]])
                        final_stats.append((j, si))
                        if len(final_stats) == 4:
                            flush_final(2)

            for kq in range(4):
                up_unit(0, kq)
            up_evac(0)
            for g in range(1, NG):
                for u in range(4):
                    up_unit(g, u)
                    down_unit(g - 1, u)
                up_evac(g)
            for jb in range(4):
                down_unit(NG - 1, jb)
            flush_final()

            stats_rstd([(PS[:, 7, :], slice(0, T), b_ps[7])], D, eps_rms, 0, None)
            for j0 in range(0, KC, 2):
                for j in (j0, j0 + 1):
                    aj = ACC[:, j, :]
                    dve(lambda e, aj=aj: e.tensor_tensor(out=aj, in0=aj, in1=RSTD[:, 0, 0:T], op=ALU.mult),
                        reads=[b_acc[j], b_rstd[0]], writes=[b_acc[j]])
                for j in (j0, j0 + 1):
                    aj = ACC[:, j, :]
                    dve(lambda e, aj=aj, j=j: e.scalar_tensor_tensor(
                            out=aj, in0=aj, scalar=cg(3, j), in1=HT[:, j, :], op0=ALU.mult, op1=ALU.add),
                        reads=[b_acc[j], b_ht[j], b_const], writes=[b_acc[j]])
                for j in (j0, j0 + 1):
                    aj = ACC[:, j, :]
                    oi = state["out"] % 4
                    state["out"] += 1
                    dst = outT_v[:, j, t * T:(t + 1) * T]
                    store_tok[j] = sch.dma("sp", lambda e, aj=aj, dst=dst: e.dma_start(out=dst, in_=aj),
                                           out_sem[oi], reads=[b_acc[j]])
            tile_fence = sch.fence()

        total_wait = sch.fence()
        with nc.Block() as block:
            def make_body(eng):
                def body(e):
                    known = {}
                    for fn, toks, sig in sch.ops[eng]:
                        for k, v in toks.items():
                            if known.get(k, 0) >= v:
                                continue
                            e.wait_ge(sems[k], v)
                            known[k] = v
                        inst = fn(e)
                        if sig is not None:
                            inst.then_inc(sems[sig[0]], sig[1])
                    if eng == "sp":
                        for k in out_sem:
                            if sch.dma_cnt.get(k, 0):
                                e.wait_ge(sems[k], sch.dma_cnt[k])
                return body
            block.tensor(make_body("pe"))
            block.scalar(make_body("act"))
            block.vector(make_body("dve"))
            block.gpsimd(make_body("pool"))
            block.sync(make_body("sp"))
    return nc


def _prep_inputs(x, norm_mix_pre, w_in, conf_dw_w, conf_dw_b, conf_ln_g, conf_ln_b,
                 sc_conv_w, w_out, norm_mix_post, norm_mlp_pre, w_up, w_down, norm_mlp_post):
    f = np.float32
    x2 = np.asarray(x, dtype=f).reshape(S, D)
    xpad = np.concatenate([np.zeros((HALO, D), f), x2], axis=0)
    cst = np.zeros((128, NCF), f)
    for n, g in enumerate([norm_mix_pre, norm_mix_post, norm_mlp_pre, norm_mlp_post]):
        cst[:, C_G + n * 32:C_G + (n + 1) * 32] = np.asarray(g, f).reshape(32, 128).T
    cst[:, C_DWW:C_DWW + 496] = np.asarray(conf_dw_w, f).reshape(31, 16, 128).transpose(2, 1, 0).reshape(128, 496)
    cst[:, C_DWB:C_DWB + 16] = np.asarray(conf_dw_b, f).reshape(16, 128).T
    cst[:, C_LNG:C_LNG + 16] = np.asarray(conf_ln_g, f).reshape(16, 128).T
    cst[:, C_LNB:C_LNB + 16] = np.asarray(conf_ln_b, f).reshape(16, 128).T
    cst[:, C_SCW:C_SCW + 48] = np.asarray(sc_conv_w, f).reshape(3, 16, 128).transpose(2, 1, 0).reshape(128, 48)
    cst[:, C_EPS] = RMS_EPS
    cst[:, C_EPS + 1] = LN_EPS
    w_in = np.ascontiguousarray(w_in, dtype=f)
    w_out = np.ascontiguousarray(w_out, dtype=f)
    w_up = np.ascontiguousarray(w_up, dtype=f)
    w_down = np.ascontiguousarray(w_down, dtype=f)
    in_maps = []
    for c in range(NCORES):
        xt = np.empty((NT, D, TW), f)
        for t in range(NT):
            s0 = c * TPC + t * T
            xt[t] = xpad[s0:s0 + TW, :].T
        in_maps.append({"xT": xt, "w_in": w_in, "w_out": w_out, "w_up": w_up, "w_down": w_down, "cst": cst})
    return in_maps


_NC_CACHE = {}


def kernel(**inputs):
    in_maps = _prep_inputs(**inputs)
    if "nc" not in _NC_CACHE:
        _NC_CACHE["nc"] = build_program()
    nc = _NC_CACHE["nc"]
    res = run_bass_kernel_spmd(nc, in_maps, core_ids=list(range(NCORES)))
    outs = [np.asarray(r["outT"], dtype=np.float32).T for r in res.results]
    return np.concatenate(outs, axis=0).reshape(1, S, D)
```
